# Optimizing a Trainium2 kernel written in Bass

```python
import jax, jax.numpy as jnp
from jax import lax
import numpy as np

D_MODEL = 1024
BATCH = 32
SEQ = 2048
DEPTH = 1
DEC_BATCH = 8
DEC_SEQ = 2048
PAST_LEN = 128

MLA_HEADS = 8
QK_NOPE = 128
QK_ROPE = 64
QK_HEAD = QK_NOPE + QK_ROPE
V_HEAD = 128
Q_LORA = 256
KV_LORA = 256
ROPE_THETA = 10000.0
Q_BLOCK = 128
CONV_DIM = 512
CONV_WIDTH = 31
D_FF = 4 * D_MODEL
EPS = 1e-6
IN_COLS = Q_LORA + KV_LORA + QK_ROPE + 2 * CONV_DIM + 2 * D_MODEL
SPLITS = [Q_LORA, Q_LORA + KV_LORA, Q_LORA + KV_LORA + QK_ROPE, Q_LORA + KV_LORA + QK_ROPE + 2 * CONV_DIM]

kernel_name = "gated_mla_conformer_encoder"


def rms_norm(x, g):
    xf = x.astype(jnp.float32)
    y = xf * lax.rsqrt(jnp.mean(xf * xf, axis=-1, keepdims=True) + EPS)
    return (y * g.astype(jnp.float32)).astype(x.dtype)


def layer_norm(x, g, b):
    xf = x.astype(jnp.float32)
    mu = jnp.mean(xf, axis=-1, keepdims=True)
    xc = xf - mu
    var = jnp.mean(xc * xc, axis=-1, keepdims=True)
    y = xc * lax.rsqrt(var + EPS) * g.astype(jnp.float32) + b.astype(jnp.float32)
    return y.astype(x.dtype)


def rope_tables(seq_len):
    inv_freq = 1.0 / (ROPE_THETA ** (jnp.arange(0, QK_ROPE, 2, dtype=jnp.float32) / QK_ROPE))
    pos = jnp.arange(seq_len, dtype=jnp.float32)
    ang = pos[:, None] * inv_freq[None, :]
    ang = jnp.concatenate([ang, ang], axis=-1)
    return jnp.cos(ang), jnp.sin(ang)


def apply_rope(x, cos, sin):
    xf = x.astype(jnp.float32)
    x1, x2 = jnp.split(xf, 2, axis=-1)
    rot = jnp.concatenate([-x2, x1], axis=-1)
    return (xf * cos[None, :, None, :] + rot * sin[None, :, None, :]).astype(x.dtype)


def blocked_attention(q, k, v):
    b, s, h, dqk = q.shape
    n_blk = s // Q_BLOCK
    qb = q.reshape(b, n_blk, Q_BLOCK, h, dqk).transpose(1, 0, 2, 3, 4)
    scale = QK_HEAD ** -0.5

    def one_block(q_blk):
        sc = jnp.einsum('bqhd,bkhd->bhqk', q_blk, k, preferred_element_type=jnp.float32) * scale
        p = jax.nn.softmax(sc, axis=-1).astype(v.dtype)
        return jnp.einsum('bhqk,bkhv->bqhv', p, v)

    out = lax.map(one_block, qb)
    return out.transpose(1, 0, 2, 3, 4).reshape(b, s, h, v.shape[-1])


def encoder_layer(x, g_mix, w_in, g_q_lat, w_uq, g_kv_lat, w_ukv, g_qk_q, g_qk_k,
                  w_o_attn, w_dw, b_dw, g_conv_ln, b_conv_ln, w_o_conv, w_out,
                  g_ffn, w_ff1, w_ff2):
    b, s, _ = x.shape
    h = rms_norm(x, g_mix)
    proj = h @ w_in
    q_lat, kv_lat, k_pe, conv_in, gate_in = jnp.split(proj, SPLITS, axis=-1)

    cos, sin = rope_tables(s)
    q = (rms_norm(q_lat, g_q_lat) @ w_uq).reshape(b, s, MLA_HEADS, QK_HEAD)
    kv = (rms_norm(kv_lat, g_kv_lat) @ w_ukv).reshape(b, s, MLA_HEADS, QK_NOPE + V_HEAD)
    k_nope, v = jnp.split(kv, [QK_NOPE], axis=-1)
    k_pe_h = jnp.broadcast_to(k_pe[:, :, None, :], (b, s, MLA_HEADS, QK_ROPE))
    k = jnp.concatenate([k_nope, k_pe_h], axis=-1)
    q = rms_norm(q, g_qk_q)
    k = rms_norm(k, g_qk_k)
    q = jnp.concatenate([q[..., :QK_NOPE], apply_rope(q[..., QK_NOPE:], cos, sin)], axis=-1)
    k = jnp.concatenate([k[..., :QK_NOPE], apply_rope(k[..., QK_NOPE:], cos, sin)], axis=-1)
    attn = blocked_attention(q, k, v).reshape(b, s, MLA_HEADS * V_HEAD)
    attn_out = attn @ w_o_attn

    a, gt = jnp.split(conv_in, 2, axis=-1)
    u = a * jax.nn.sigmoid(gt)
    u = lax.conv_general_dilated(
        u, w_dw, window_strides=(1,),
        padding=[(CONV_WIDTH // 2, CONV_WIDTH // 2)],
        dimension_numbers=('NWC', 'WIO', 'NWC'),
        feature_group_count=CONV_DIM) + b_dw
    u = jax.nn.silu(layer_norm(u, g_conv_ln, b_conv_ln))
    conv_out = u @ w_o_conv

    gate_a, gate_c = jnp.split(jax.nn.sigmoid(gate_in), 2, axis=-1)
    x = x + (gate_a * attn_out + gate_c * conv_out) @ w_out

    hf = rms_norm(x, g_ffn)
    x = x + jnp.square(jax.nn.relu(hf @ w_ff1)) @ w_ff2
    return x


def setup_inputs(seed: int = 0) -> dict:
    key = jax.random.key(seed)
    ks = jax.random.split(key, 20)

    def w(k, shape, fan_in):
        return jax.random.normal(k, shape, jnp.float32) * (fan_in ** -0.5)

    def gain(k, shape):
        return 1.0 + 0.05 * jax.random.normal(k, shape, jnp.float32)

    def bias(k, shape):
        return 0.01 * jax.random.normal(k, shape, jnp.float32)

    L = DEPTH
    return {
        "x_prompt": jax.random.normal(ks[0], (BATCH, SEQ, D_MODEL), jnp.float32),
        "x_sample": jax.random.normal(ks[1], (DEC_BATCH, DEC_SEQ, D_MODEL), jnp.float32),
        "g_mix": gain(ks[2], (L, D_MODEL)),
        "w_in": w(ks[3], (L, D_MODEL, IN_COLS), D_MODEL),
        "g_q_lat": gain(ks[4], (L, Q_LORA)),
        "w_uq": w(ks[5], (L, Q_LORA, MLA_HEADS * QK_HEAD), Q_LORA),
        "g_kv_lat": gain(ks[6], (L, KV_LORA)),
        "w_ukv": w(ks[7], (L, KV_LORA, MLA_HEADS * (QK_NOPE + V_HEAD)), KV_LORA),
        "g_qk_q": gain(ks[8], (L, QK_HEAD)),
        "g_qk_k": gain(ks[9], (L, QK_HEAD)),
        "w_o_attn": w(ks[10], (L, MLA_HEADS * V_HEAD, D_MODEL), MLA_HEADS * V_HEAD),
        "w_dw": w(ks[11], (L, CONV_WIDTH, 1, CONV_DIM), CONV_WIDTH),
        "b_dw": bias(ks[12], (L, CONV_DIM)),
        "g_conv_ln": gain(ks[13], (L, CONV_DIM)),
        "b_conv_ln": bias(ks[14], (L, CONV_DIM)),
        "w_o_conv": w(ks[15], (L, CONV_DIM, D_MODEL), CONV_DIM),
        "w_out": w(ks[16], (L, D_MODEL, D_MODEL), D_MODEL),
        "g_ffn": gain(ks[17], (L, D_MODEL)),
        "w_ff1": w(ks[18], (L, D_MODEL, D_FF), D_MODEL),
        "w_ff2": w(ks[19], (L, D_FF, D_MODEL), D_FF),
    }


def reference(x_prompt, x_sample, g_mix, w_in, g_q_lat, w_uq, g_kv_lat, w_ukv, g_qk_q, g_qk_k,
              w_o_attn, w_dw, b_dw, g_conv_ln, b_conv_ln, w_o_conv, w_out, g_ffn, w_ff1, w_ff2):
    y_prompt = x_prompt
    y_sample = x_sample
    for l in range(DEPTH):
        params = (g_mix[l], w_in[l], g_q_lat[l], w_uq[l], g_kv_lat[l], w_ukv[l], g_qk_q[l], g_qk_k[l],
                  w_o_attn[l], w_dw[l], b_dw[l], g_conv_ln[l], b_conv_ln[l], w_o_conv[l], w_out[l],
                  g_ffn[l], w_ff1[l], w_ff2[l])
        y_prompt = encoder_layer(y_prompt, *params)
        y_sample = encoder_layer(y_sample, *params)
    return (y_prompt, y_sample)
```

```python
import numpy as np
from contextlib import ExitStack
import concourse.bass as bass
import concourse.mybir as mybir
from concourse.bass_utils import run_bass_kernel_spmd

F32 = mybir.dt.float32
BF16 = mybir.dt.bfloat16
AF = mybir.ActivationFunctionType
ALU = mybir.AluOpType

D = 1024
H = 8
QKH = 192
QL = 256
KVL = 256
CD = 512
CW = 31
DFF = 4096
EPS = 1e-6
INC = 3648
QB = 512
NCORES = 8
RING = 3
RECIP_ACT = True
QUAD = None


def _dsize(dt):
    return mybir.dt.size(dt)


class Op:
    __slots__ = ("eng", "fn", "waits", "sig", "sigval", "dma_key", "dma_val")


class Sched:
    ENGS = ("pe", "act", "dve", "pool", "sp")

    def __init__(self, nc, es, tag):
        self.nc = nc
        self.es = es
        self.tag = tag
        self.streams = {e: [] for e in self.ENGS}
        self.recs = {}
        self.dma_cnt = {}
        self.nops = 0

    @staticmethod
    def region(ap):
        name = ap.tensor.name
        sp = str(ap.space)
        if "DRAM" in sp.upper() or "HBM" in sp.upper():
            return None
        apl = ap.ap
        size = _dsize(ap.dtype)
        pstep, npart = apl[0]
        p0 = ap.offset // pstep
        f0 = ap.offset % pstep
        ext = 1 + sum((c - 1) * abs(s) for s, c in apl[1:])
        lo = f0 * size
        hi = (f0 + ext) * size
        if "PS" in sp.upper():
            lo = lo // 2048 * 2048
            hi = -(-hi // 2048) * 2048
            return (name, lo, hi, 0, 128)
        return (name, lo, hi, p0, p0 + npart)

    def _scan(self, reg, is_write, deps, eng=None):
        name, lo, hi, p0, p1 = reg
        rr = name == "psum"
        for r in self.recs.get(name, ()):
            if r[0] < hi and lo < r[1] and r[2] < p1 and p0 < r[3]:
                if is_write or r[5] or (rr and r[4].eng != eng):
                    kind = "raw" if (r[5] and not is_write) else "other"
                    prev = deps.get(r[4])
                    if prev is None or kind == "raw":
                        deps[r[4]] = kind

    def _record(self, reg, op, is_write):
        name, lo, hi, p0, p1 = reg
        lst = self.recs.setdefault(name, [])
        if is_write:
            lst[:] = [r for r in lst if not (lo <= r[0] and r[1] <= hi and p0 <= r[2] and r[3] <= p1)]
        else:
            lst[:] = [r for r in lst if not (not r[5] and r[4].eng == op.eng and r[4].dma_key is None
                                             and op.dma_key is None
                                             and r[0] == lo and r[1] == hi and r[2] == p0 and r[3] == p1)]
        lst.append((lo, hi, p0, p1, op, is_write))

    def add(self, eng, fn, reads=(), writes=(), dma_key=None):
        op = Op()
        op.eng = eng
        op.fn = fn
        op.sig = False
        op.sigval = 0
        op.dma_key = dma_key
        op.dma_val = 0
        rregs = [g for g in (self.region(a) for a in reads) if g is not None]
        wregs = [g for g in (self.region(a) for a in writes) if g is not None]
        deps = {}
        for g in rregs:
            self._scan(g, False, deps, eng)
        for g in wregs:
            self._scan(g, True, deps, eng)
        waits = []
        for p, kind in deps.items():
            if p.dma_key is not None:
                waits.append(("dma", p.dma_key, 16 * self.dma_cnt[p.dma_key]))
            else:
                if p.eng == eng and eng == "pe":
                    continue
                p.sig = True
                waits.append(("eng", p))
        op.waits = waits
        for g in rregs:
            self._record(g, op, False)
        for g in wregs:
            self._record(g, op, True)
        if dma_key is not None:
            self.dma_cnt[dma_key] = self.dma_cnt.get(dma_key, 0) + 1
            op.dma_val = 16 * self.dma_cnt[dma_key]
        self.streams[eng].append(op)
        self.nops += 1
        return op

    def mm(self, out, lhsT, rhs, start, stop):
        self.add("pe", lambda e: e.matmul(out, lhsT=lhsT, rhs=rhs, start=start, stop=stop),
                 reads=[lhsT, rhs], writes=[out])

    def tr(self, out, in_, ident):
        self.add("pe", lambda e: e.transpose(out=out, in_=in_, identity=ident),
                 reads=[in_, ident], writes=[out])

    def act(self, out, in_, func, scale=1.0, bias=0.0, accum=None):
        reads = [in_]
        if not isinstance(scale, (int, float)):
            reads.append(scale)
        if not isinstance(bias, (int, float)):
            reads.append(bias)
        writes = [out] + ([accum] if accum is not None else [])
        kw = {}
        if accum is not None:
            kw["accum_out"] = accum
        self.add("act", lambda e: e.activation(out=out, in_=in_, func=func, scale=scale, bias=bias, **kw),
                 reads=reads, writes=writes)

    def rsqrt_act(self, out, in_, scale, eps):
        self.act(out, in_, AF.Ln, scale=scale, bias=eps)
        self.act(out, out, AF.Exp, scale=-0.5)

    def ts(self, eng, out, in0, s1, op0, s2=None, op1=None):
        reads = [in0]
        if not isinstance(s1, (int, float)):
            reads.append(s1)
        if s2 is not None and not isinstance(s2, (int, float)):
            reads.append(s2)
        kw = {}
        if op1 is not None:
            kw["op1"] = op1
        self.add(eng, lambda e: e.tensor_scalar(out=out, in0=in0, scalar1=s1, scalar2=s2, op0=op0, **kw),
                 reads=reads, writes=[out])

    def tt(self, eng, out, in0, in1, op):
        self.add(eng, lambda e: e.tensor_tensor(out=out, in0=in0, in1=in1, op=op),
                 reads=[in0, in1], writes=[out])

    def stt(self, out, in0, scalar, in1, op0, op1):
        reads = [in0, in1]
        if not isinstance(scalar, (int, float)):
            reads.append(scalar)
        self.add("dve", lambda e: e.scalar_tensor_tensor(out=out, in0=in0, scalar=scalar, in1=in1, op0=op0, op1=op1),
                 reads=reads, writes=[out])

    def copy(self, eng, out, in_):
        if eng == "act":
            self.add("act", lambda e: e.activation(out=out, in_=in_, func=AF.Copy), reads=[in_], writes=[out])
        else:
            self.add(eng, lambda e: e.tensor_copy(out=out, in_=in_), reads=[in_], writes=[out])

    def recip(self, out, in_, on_dve=False):
        if RECIP_ACT and not on_dve:
            self.act(out, in_, AF.Ln)
            self.act(out, out, AF.Exp, scale=-1.0)
        else:
            self.add("dve", lambda e: e.reciprocal(out=out, in_=in_), reads=[in_], writes=[out])

    def memset(self, eng, out, val):
        self.add(eng, lambda e: e.memset(out, val), writes=[out])

    def dma(self, out, in_, key, reads=(), writes=(), slow=False):
        kw = {"allow_slow_non_contiguous": True} if slow else {}
        self.add("sp", lambda e: e.dma_start(out=out, in_=in_, **kw), reads=reads, writes=writes, dma_key=key)

    def final_wait(self):
        op = Op()
        op.eng = "sp"
        op.fn = None
        op.sig = False
        op.sigval = 0
        op.dma_key = None
        op.dma_val = 0
        op.waits = [("dma", k, 16 * c) for k, c in self.dma_cnt.items()]
        self.streams["sp"].append(op)

    def emit(self):
        nc = self.nc
        es = self.es
        esem = {e: es.enter_context(nc.semaphore(f"{self.tag}_e_{e}")) for e in self.ENGS if e != "sp"}
        dsem = {}
        for i, k in enumerate(self.dma_cnt):
            dsem[k] = es.enter_context(nc.semaphore(f"{self.tag}_d{i}"))
        for e, ops in self.streams.items():
            n = 0
            for op in ops:
                if op.dma_key is None and op.sig:
                    n += 1
                    op.sigval = n

        def run(ename, eng):
            waited = {}
            for op in self.streams[ename]:
                for w in op.waits:
                    if w[0] == "dma":
                        sem, val, sid = dsem[w[1]], w[2], ("d", w[1])
                    else:
                        p = w[1]
                        sem, val, sid = esem[p.eng], p.sigval, ("e", p.eng)
                    if val > waited.get(sid, 0):
                        eng.wait_ge(sem, val)
                        waited[sid] = val
                if op.fn is None:
                    continue
                inst = op.fn(eng)
                if op.dma_key is not None:
                    inst.then_inc(dsem[op.dma_key], 16)
                elif op.sig:
                    inst.then_inc(esem[ename], 1)

        with nc.Block() as block:
            @block.tensor
            def _(e):
                run("pe", e)

            @block.scalar
            def _(e):
                run("act", e)

            @block.vector
            def _(e):
                run("dve", e)

            @block.gpsimd
            def _(e):
                run("pool", e)

            @block.sync
            def _(e):
                run("sp", e)


class BufPool:
    def __init__(self, items):
        self.free = list(items)

    def get(self):
        assert self.free, "buffer pool exhausted (emission-order bug)"
        return self.free.pop(0)

    def put(self, x):
        self.free.append(x)

    def take(self, x):
        for i, f in enumerate(self.free):
            if f is x:
                return self.free.pop(i)
        raise AssertionError("requested buffer not free (emission-order bug)")


class WStream:
    def __init__(self, sch, ring, units):
        self.sch = sch
        self.ring = ring
        self.units = units
        self.R = ring.shape[1]
        self.freeslots = list(range(self.R))
        self.next_load = 0
        self.next_get = 0
        self.loaded = {}

    def _pump(self):
        while self.freeslots and self.next_load < len(self.units):
            s = self.freeslots.pop(0)
            src = self.units[self.next_load]
            n = src.shape[1]
            self.sch.dma(out=self.ring[:, s, 0:n], in_=src, key=("w", s), writes=[self.ring[:, s, :]])
            self.loaded[self.next_load] = s
            self.next_load += 1

    def get(self):
        self._pump()
        i = self.next_get
        self.next_get += 1
        assert i in self.loaded, "weight ring too small for simultaneous units"
        s = self.loaded[i]
        return (i, self.ring[:, s, :])

    def done(self, h):
        s = self.loaded.pop(h[0])
        self.freeslots.append(s)
        self._pump()


def build_program(nseq, S, debug=(), stop=None):
    NT = S // 128
    NB = S // QB
    assert S % QB == 0
    nc = bass.Bass("TRN2", target_bir_lowering=False)

    def din(name, shape):
        return nc.dram_tensor(name, list(shape), F32, kind="ExternalInput").ap()

    x_d = din("x", [nseq, S, D])
    g_mix = din("g_mix", [D])
    w_in = din("w_in", [D, INC])
    g_q_lat = din("g_q_lat", [QL])
    w_uq = din("w_uq", [QL, H * QKH])
    g_kv_lat = din("g_kv_lat", [KVL])
    w_ukv = din("w_ukv", [KVL, H * 256])
    g_qk_q = din("g_qk_q", [QKH])
    g_qk_k = din("g_qk_k", [QKH])
    w_o_attn = din("w_o_attn", [D, D])
    w_dw = din("w_dw", [CW, CD])
    b_dw = din("b_dw", [CD])
    g_conv_ln = din("g_conv_ln", [CD])
    b_conv_ln = din("b_conv_ln", [CD])
    w_o_conv = din("w_o_conv", [CD, D])
    w_out = din("w_out", [D, D])
    g_ffn = din("g_ffn", [D])
    w_ff1 = din("w_ff1", [D, DFF])
    w_ff2 = din("w_ff2", [DFF, D])
    cos_d = din("rope_cos", [S, 64])
    sin_d = din("rope_sin", [S, 64])
    y_d = nc.dram_tensor("y", [nseq, S, D], F32, kind="ExternalOutput").ap()

    def scr(name, shape, dt=BF16):
        return nc.dram_tensor(name, list(shape), dt, kind="Internal").ap()

    sA1 = scr("sA1", [128, 8 * 320])
    sA2 = scr("sA2", [2, 128, 4096])
    sA3 = scr("sA3", [128, 4096])
    sQ1 = scr("sQ1", [128, 2048])
    sQ2 = scr("sQ2", [128, 3072])
    sM = scr("sM", [8, 128, 3584])
    sO = scr("sO", [2, 128, 4096])
    sF1 = scr("sF1", [8, 128, 4096])
    sF2 = scr("sF2", [8, 128, 4096])
    sTK = scr("sTK", [128, NT, 2, 64], F32)
    sTQ = scr("sTQ", [128, NT, 2, 64], F32)

    dbg_out = {}

    with ExitStack() as es:
        E = es.enter_context
        NSTG = 6
        stg = E(nc.sbuf_tensor("stg", [128, NSTG, 4096], F32))
        stb = E(nc.sbuf_tensor("stb", [128, NSTG, 4096], BF16))
        gc = E(nc.sbuf_tensor("gc", [128, 20], F32))
        tcs = E(nc.sbuf_tensor("tcs", [128, 2, NT, 64], F32))
        gro = E(nc.sbuf_tensor("gro", [128, 4, 64], F32))
        tqk = E(nc.sbuf_tensor("tqk", [128, 2, NT, 2, 64], F32))
        sch = Sched(nc, es, "p")

        sch.dma(out=gc[:, 0:8], in_=g_mix.rearrange("(k p) -> p k", p=128), key="gc", writes=[gc[:]], slow=True)
        sch.dma(out=gc[:, 8:16], in_=g_ffn.rearrange("(k p) -> p k", p=128), key="gc", writes=[gc[:]], slow=True)
        sch.dma(out=gc[:, 16:18], in_=g_q_lat.rearrange("(k p) -> p k", p=128), key="gc", writes=[gc[:]], slow=True)
        sch.dma(out=gc[:, 18:20], in_=g_kv_lat.rearrange("(k p) -> p k", p=128), key="gc", writes=[gc[:]], slow=True)
        sch.dma(out=tcs[:, 0, :, :], in_=cos_d.rearrange("(n p) d -> p n d", p=128), key="tc0", writes=[tcs[:, 0, :, :]])
        sch.dma(out=tcs[:, 1, :, :], in_=sin_d.rearrange("(n p) d -> p n d", p=128), key="tc1", writes=[tcs[:, 1, :, :]])
        for qi, gvec in enumerate((g_qk_q, g_qk_k)):
            sch.dma(out=gro[:, 2 * qi, :], in_=gvec[128:192].partition_broadcast(128), key="gro", writes=[gro[:]])
            sch.dma(out=gro[:, 2 * qi + 1, 0:32], in_=gvec[160:192].partition_broadcast(128), key="gro", writes=[gro[:]])
            sch.dma(out=gro[:, 2 * qi + 1, 32:64], in_=gvec[128:160].partition_broadcast(128), key="gro", writes=[gro[:]])
        for qi, dst in enumerate((sTQ, sTK)):
            for cs in range(2):
                sch.tt("dve", tqk[:, qi, :, cs, :], tcs[:, cs, :, :],
                       gro[:, 2 * qi + cs:2 * qi + cs + 1, :].broadcast_to([128, NT, 64]), ALU.mult)
            sch.dma(out=dst, in_=tqk[:, qi, :, :, :], key=("tqk", qi), reads=[tqk[:, qi, :, :, :]])

        cnt = [0]

        def prep(src, ncols, scale, stores, src_view=None):
            i = cnt[0] % NSTG
            cnt[0] += 1
            dst = stg[:, i, 0:ncols] if src_view is None else src_view(stg[:, i, 0:ncols])
            sch.dma(out=dst, in_=src, key=("stg", i), writes=[stg[:, i, :]])
            if scale is not None:
                sch.ts("dve", stb[:, i, 0:ncols], stg[:, i, 0:ncols], scale, ALU.mult)
            else:
                sch.copy(("act", "dve", "act", "pool")[cnt[0] % 4], stb[:, i, 0:ncols], stg[:, i, 0:ncols])
            for sv, dv in stores(stb[:, i, 0:ncols]):
                sch.dma(out=dv, in_=sv, key=("stb", i), reads=[stb[:, i, :]])

        sA2v = sA2.rearrange("u p (j g k n) -> p u j g k n", j=2, g=2, k=8)
        sMv = sM.rearrange("c p (b n) -> p c b n", n=128)
        sF1v = sF1.rearrange("u p (m k n) -> p u m k n", m=4, k=8)
        for k in range(8):
            def st_win(v, k=k):
                out = [(v[:, 0:256], sQ1[:, k * 256:(k + 1) * 256]),
                       (v[:, 256:576], sA1[:, k * 320:(k + 1) * 320])]
                for g in range(2):
                    for u in range(2):
                        out.append((v[:, 576 + g * 512 + u * 256:576 + g * 512 + (u + 1) * 256].rearrange("p (j n) -> p j n", j=2),
                                    sA2v[:, u, :, g, k, :]))
                for g in range(2):
                    out.append((v[:, 1600 + g * 1024:1600 + (g + 1) * 1024].rearrange("p (c n) -> p c n", n=128),
                                sMv[:, :, 12 + 8 * g + k, :]))
                return out
            prep(w_in[k * 128:(k + 1) * 128, :], INC, gc[:, k:k + 1], st_win)
        for k in range(2):
            prep(w_uq[k * 128:(k + 1) * 128, :], 1536, gc[:, 16 + k:17 + k],
                 lambda v, k=k: [(v, sQ2[:, k * 1536:(k + 1) * 1536])])
        for k in range(2):
            def st_ukv(v, k=k):
                v4 = v.rearrange("p (h g n) -> p h g n", h=8, g=2)
                d4 = sA3[:, k * 2048:(k + 1) * 2048].rearrange("p (g h n) -> p g h n", g=2, h=8)
                return [(v4[:, :, g, :], d4[:, g, :, :]) for g in range(2)]
            prep(w_ukv[k * 128:(k + 1) * 128, :], 2048, gc[:, 18 + k:19 + k], st_ukv)
        for k in range(8):
            prep(w_o_attn[k * 128:(k + 1) * 128, :], 1024, None,
                 lambda v, k=k: [(v.rearrange("p (c n) -> p c n", n=128), sMv[:, :, k, :])])
        for k in range(4):
            prep(w_o_conv[k * 128:(k + 1) * 128, :], 1024, None,
                 lambda v, k=k: [(v.rearrange("p (c n) -> p c n", n=128), sMv[:, :, 8 + k, :])])
        sOv = sO.rearrange("h p (k n) -> p h k n", k=8)
        for k in range(8):
            prep(w_out[k * 128:(k + 1) * 128, :], 1024, None,
                 lambda v, k=k: [(v.rearrange("p (h n) -> p h n", h=2), sOv[:, :, k, :])])
        for k in range(8):
            prep(w_ff1[k * 128:(k + 1) * 128, :], 4096, gc[:, 8 + k:9 + k],
                 lambda v, k=k: [(v.rearrange("p (u m n) -> p u m n", u=8, m=4)[:, :, m, :], sF1v[:, :, m, k, :])
                                 for m in range(4)])
        for u in range(8):
            prep(w_ff2[u * 512:(u + 1) * 512, :].rearrange("(kk p) n -> p kk n", p=128), 4096, None,
                 lambda v, u=u: [(v, sF2[u])],
                 src_view=lambda a: a.rearrange("p (kk n) -> p kk n", kk=4))
        sch.final_wait()
        sch.emit()

    with ExitStack() as es:
        E = es.enter_context

        def sb(name, shape, dt):
            return E(nc.sbuf_tensor(name, list(shape), dt))

        KTn = sb("KTn", [128, 8, S], BF16)
        KTr = sb("KTr", [128, 2, S], BF16)
        Vt = sb("Vt", [128, NT, 1024], BF16)
        uT = sb("uT", [128, 4, S + 32], BF16)
        rk = sb("rk", [128, NT, 8], F32)
        wring = sb("wring", [128, RING, 4096], BF16)
        big = sb("big", [128, 16384], BF16)
        xT = sb("xT", [128, 8, QB], BF16)
        x1buf = sb("x1buf", [128, 4, 1024], F32)
        xn = sb("xn", [128, 2, 1024], BF16)
        latn = sb("latn", [128, 4, 256], BF16)
        latT = sb("latT", [128, 2, QB], BF16)
        tb = sb("tb", [128, 2048], BF16)
        qr = sb("qr", [128, 8, 64], BF16)
        krd = sb("krd", [128, 4, 128], BF16)
        tabs = sb("tabs", [128, 4, 2, 64], F32)
        gm = sb("gm", [128, 4, QB], F32)
        gsb = gm[:, 0:2, :]
        mtmp = gm[:, 2:4, :]
        cacc = gm
        psm = sb("psm", [128, 4, QB], BF16) if QUAD else None
        qlT_all = sb("qlT_all", [128, 2, S], BF16)
        latq = sb("latq", [128, 4, 256], BF16)
        rc = tb[:, 0:1024].bitcast(F32)
        lnT = sb("lnT", [128, 2, 512], BF16)
        junk = lnT[:, :, :].rearrange("p a n -> p (a n)")
        st = sb("st", [128, 256], F32)
        cst = sb("cst", [128, 64], F32)
        wdc = sb("wdc", [128, 4, CW], F32)
        ident = sb("ident", [128, 128], BF16)
        identf = sb("identf", [128, 128], F32)
        ones = sb("ones", [128, 128], BF16)
        psum = E(nc.psum_tensor("psum", [128, 8, 512], F32))

        hT = big[:, :].rearrange("p (k n) -> p k n", n=QB)
        QTn = big[:, 0:4096].rearrange("p (h n) -> p h n", n=QB)
        QTr = big[:, 4096:6144].rearrange("p (h n) -> p h n", n=QB)
        pring_v = big[:, 6144:10240].rearrange("p (h n) -> p h n", n=QB)
        attnT = big[:, 10240:14336].rearrange("p (h n) -> p h n", n=QB)
        uact = big[:, 14336:16384].rearrange("p (h n) -> p h n", n=QB)
        mT = QTn

        ss4 = st[:, 0:4]
        tmp4 = st[:, 4:8]
        rs4 = st[:, 8:12]
        ssQA = st[:, 216:224]
        tmpQA = st[:, 224:232]
        rsQA = st[:, 232:240]
        ssQ = ssQA[:, 0:4]
        ssA = ssQA[:, 4:8]
        rsQ = rsQA[:, 0:4]
        rsA = rsQA[:, 4:8]
        sskpe = st[:, 24:28]
        ssk = st[:, 32:64].rearrange("p (t h) -> p t h", h=8)
        zk = st[:, 64:96].rearrange("p (t h) -> p t h", h=8)
        ssq = st[:, 96:128].rearrange("p (t h) -> p t h", h=8)
        tmpq = st[:, 128:160].rearrange("p (t h) -> p t h", h=8)
        rq = st[:, 160:192].rearrange("p (t h) -> p t h", h=8)
        lsum = st[:, 192:196]
        lssq = st[:, 196:200]
        lmean = st[:, 200:204]
        lmsq = st[:, 204:208]
        lvar = st[:, 208:212]
        lrs = st[:, 212:216]
        mhalf = cst[:, 0:32]
        gqn = cst[:, 32:33]
        gkn = cst[:, 33:34]
        gln = cst[:, 36:40]
        bln = cst[:, 40:44]
        bdw = cst[:, 44:48]

        sch = Sched(nc, es, "m")
        pbank = [psum[:, i, :] for i in range(8)]
        ps = BufPool(pbank)
        pring = BufPool([pring_v[:, i, :] for i in range(8)])

        units = []
        for s in range(nseq):
            for b in range(NB):
                units += [sQ1, sA1, sA2[0], sA2[1], sA3]
            for b in range(NB):
                units += [sQ2] + [sM[c] for c in range(8)] + [sO[0], sO[1]]
                units += [sF1[u] for u in range(8)] + [sF2[u] for u in range(8)]
        W = WStream(sch, wring, units)

        def dbg(name, ap, dt=None):
            if name not in debug or name in dbg_out:
                return
            dt = dt or ap.dtype
            t = nc.dram_tensor("dbg_" + name, list(ap.shape), dt, kind="ExternalOutput").ap()
            dbg_out[name] = t
            sch.dma(out=t, in_=ap, key="dbg", reads=[ap])

        sch.memset("pool", ident[:], 0.0)
        sch.add("pool", lambda e: e.affine_select(out=ident[:], in_=ident[:], compare_op=ALU.not_equal, fill=1.0,
                                                  base=0, pattern=[[-1, 128]], channel_multiplier=1),
                reads=[ident[:]], writes=[ident[:]])
        sch.memset("pool", identf[:], 0.0)
        sch.add("pool", lambda e: e.affine_select(out=identf[:], in_=identf[:], compare_op=ALU.not_equal, fill=1.0,
                                                  base=0, pattern=[[-1, 128]], channel_multiplier=1),
                reads=[identf[:]], writes=[identf[:]])
        sch.memset("pool", ones[:], 1.0)
        sch.memset("pool", cst[:, 0:32], -0.5)
        sch.memset("pool", uT[:], 0.0)
        sch.memset("pool", KTr[:], 0.0)
        sch.dma(out=cst[:, 32:33], in_=g_qk_q[0:128].rearrange("(p o) -> p o", o=1), key="cst", writes=[cst[:, 32:64]], slow=True)
        sch.dma(out=cst[:, 33:34], in_=g_qk_k[0:128].rearrange("(p o) -> p o", o=1), key="cst", writes=[cst[:, 32:64]], slow=True)
        sch.dma(out=cst[:, 36:40], in_=g_conv_ln.rearrange("(i p) -> p i", p=128), key="cst", writes=[cst[:, 32:64]], slow=True)
        sch.dma(out=cst[:, 40:44], in_=b_conv_ln.rearrange("(i p) -> p i", p=128), key="cst", writes=[cst[:, 32:64]], slow=True)
        sch.dma(out=cst[:, 44:48], in_=b_dw.rearrange("(i p) -> p i", p=128), key="cst", writes=[cst[:, 32:64]], slow=True)
        wdr = mtmp[0:CW, 0, :]
        sch.dma(out=wdr, in_=w_dw, key="wdr", writes=[wdr])
        bk = ps.get()
        for i in range(4):
            sch.tr(bk[:, i * CW:(i + 1) * CW], mtmp[0:CW, 0, i * 128:(i + 1) * 128], identf[0:CW, 0:CW])
        sch.copy("dve", wdc[:], bk[:, 0:4 * CW].rearrange("p (i j) -> p i j", i=4))
        ps.put(bk)

        def load_x(s, b):
            for t in range(4):
                r0 = b * QB + t * 128
                sch.dma(out=x1buf[:, t, :], in_=x_d[s, r0:r0 + 128, :], key=("x", t), writes=[x1buf[:, t, :]])

        def norm_stats():
            for t in range(4):
                sch.act(junk[:, :], x1buf[:, t, :], AF.Square, accum=ss4[:, t:t + 1])
            sch.rsqrt_act(rs4, ss4, 1.0 / D, EPS)

        def norm_stats_tile(t):
            sch.act(junk[:, :], x1buf[:, t, :], AF.Square, accum=ss4[:, t:t + 1])
            sch.rsqrt_act(rs4[:, t:t + 1], ss4[:, t:t + 1], 1.0 / D, EPS)

        def norm_tile(t):
            i = t % 2
            sch.ts("dve", xn[:, i, :], x1buf[:, t, :], rs4[:, t:t + 1], ALU.mult)
            bT = ps.get()
            bTb = bT.bitcast(BF16)
            for k in range(8):
                sch.tr(bTb[:, k * 128:(k + 1) * 128], xn[:, i, k * 128:(k + 1) * 128], ident[:])
            sch.copy("act" if t % 2 else "dve", xT[:, :, t * 128:(t + 1) * 128],
                     bTb.rearrange("p (k n) -> p k n", n=128))
            ps.put(bT)

        def norm_T():
            norm_stats()
            for t in range(4):
                norm_tile(t)

        def phaseA(s, b):
            tok0 = b * QB
            sch.dma(out=tabs[:], in_=sTK[:, b * 4:(b + 1) * 4, :, :], key="tab", writes=[tabs[:]])
            has_next = b + 1 < NB
            if b == 0:
                load_x(s, 0)
                norm_T()
                if has_next:
                    load_x(s, 1)
            wQ1 = W.get()
            wA1 = W.get()
            banks = []
            bky = ps.get()
            for t in range(4):
                bk = ps.get()
                banks.append(bk)
                xs = xT[:, :, t * 128:(t + 1) * 128]
                for k in range(8):
                    sch.mm(bk[:, 0:256], xs[:, k, :], wQ1[1][:, k * 256:(k + 1) * 256], k == 0, k == 7)
                for k in range(8):
                    sch.mm(bk[:, 256:512], xs[:, k, :], wA1[1][:, k * 320:k * 320 + 256], k == 0, k == 7)
                for k in range(8):
                    sch.mm(bky[:, t * 64:(t + 1) * 64], xs[:, k, :], wA1[1][:, k * 320 + 256:(k + 1) * 320], k == 0, k == 7)
                sch.act(junk[:, 0:256], bk[:, 0:256], AF.Square, accum=ssQ[:, t:t + 1])
                sch.act(junk[:, 256:512], bk[:, 256:512], AF.Square, accum=ssA[:, t:t + 1])
                sch.act(junk[:, 512:576], bky[:, t * 64:(t + 1) * 64], AF.Square, accum=sskpe[:, t:t + 1])
            W.done(wQ1)
            W.done(wA1)
            sch.rsqrt_act(rsQA, ssQA, 1.0 / KVL, EPS)
            t1 = mtmp[:, 0, 0:256].rearrange("p (t d) -> p t d", d=64)
            t2 = mtmp[:, 0, 256:512].rearrange("p (t d) -> p t d", d=64)
            ky4 = bky[:, 0:256].rearrange("p (t d) -> p t d", d=64)
            sch.tt("dve", t1, ky4, tabs[:, :, 0, :], ALU.mult)
            sch.tt("dve", t2[:, :, 0:32], ky4[:, :, 32:64], tabs[:, :, 1, 0:32], ALU.mult)
            sch.tt("dve", t2[:, :, 32:64], ky4[:, :, 0:32], tabs[:, :, 1, 32:64], ALU.mult)
            ps.put(bky)
            krd4 = krd[:, :, :].rearrange("p t (r d) -> p t r d", r=2)
            sch.tt("dve", krd4[:, :, 0, :], t1, t2, ALU.add)
            sch.tt("dve", krd4[:, :, 1, :], t1, t2, ALU.add)
            for t in range(4):
                bk = banks[t]
                sch.ts("dve", latn[:, t, :], bk[:, 256:512], rsA[:, t:t + 1], ALU.mult)
                sch.ts("dve", latq[:, t, :], bk[:, 0:256], rsQ[:, t:t + 1], ALU.mult)
                ps.put(bk)

            def conv_pair(wA2, j, i):
                ba = ps.get()
                bg = ps.get()
                for k in range(8):
                    o = ((j * 2 + 1) * 8 + k) * 128
                    sch.mm(bg[:, :], wA2[1][:, o:o + 128], xT[:, k, :], k == 0, k == 7)
                for k in range(8):
                    o = ((j * 2 + 0) * 8 + k) * 128
                    sch.mm(ba[:, :], wA2[1][:, o:o + 128], xT[:, k, :], k == 0, k == 7)
                sch.act(gsb[:, j, :], bg[:, :], AF.Sigmoid)
                sch.tt("dve", uT[:, i, 16 + tok0:16 + tok0 + QB], ba[:, :], gsb[:, j, :], ALU.mult)
                ps.put(ba)
                ps.put(bg)

            def lat_T(t):
                bT = ps.get()
                bTb = bT.bitcast(BF16)
                sch.tr(bTb[:, 0:128], latn[:, t, 0:128], ident[:])
                sch.tr(bTb[:, 128:256], latn[:, t, 128:256], ident[:])
                sch.tr(bTb[:, 256:384], krd[:, t, :], ident[:])
                sch.tr(bTb[:, 384:512], latq[:, t, 0:128], ident[:])
                sch.tr(bTb[:, 512:640], latq[:, t, 128:256], ident[:])
                sch.copy("act", latT[:, :, t * 128:(t + 1) * 128], bTb[:, 0:256].rearrange("p (c n) -> p c n", n=128))
                sch.copy("act", KTr[0:64, 0, tok0 + t * 128:tok0 + (t + 1) * 128], bTb[0:64, 256:384])
                sch.copy("act", KTr[64:128, 1, tok0 + t * 128:tok0 + (t + 1) * 128], bTb[64:128, 256:384])
                sch.copy("dve", qlT_all[:, :, tok0 + t * 128:tok0 + (t + 1) * 128],
                         bTb[:, 384:640].rearrange("p (c n) -> p c n", n=128))
                ps.put(bT)

            wA2a = W.get()
            conv_pair(wA2a, 0, 0)
            lat_T(0)
            lat_T(1)
            conv_pair(wA2a, 1, 1)
            W.done(wA2a)
            lat_T(2)
            lat_T(3)
            wA2b = W.get()
            wA3 = W.get()

            def kv_front(t):
                tile = b * 4 + t
                bks = [ps.get(), ps.get()]
                bvs = [ps.get(), ps.get()]
                for hf in range(2):
                    for c in range(2):
                        sch.mm(bks[hf][:, :], latT[:, c, t * 128:(t + 1) * 128],
                               wA3[1][:, c * 2048 + hf * 512:c * 2048 + (hf + 1) * 512], c == 0, c == 1)
                for hf in range(2):
                    for c in range(2):
                        sch.mm(bvs[hf][:, :], latT[:, c, t * 128:(t + 1) * 128],
                               wA3[1][:, c * 2048 + 1024 + hf * 512:c * 2048 + 1024 + (hf + 1) * 512], c == 0, c == 1)
                for h in range(8):
                    sch.act(junk[:, h * 128:(h + 1) * 128], bks[h // 4][:, (h % 4) * 128:(h % 4 + 1) * 128], AF.Square,
                            accum=ssk[:, t, h:h + 1])
                o = (t % 2) * 1024
                for hf in range(2):
                    sch.copy("dve", tb[:, o + hf * 512:o + (hf + 1) * 512], bks[hf][:, :])
                    sch.copy("act", Vt[:, tile, hf * 512:(hf + 1) * 512], bvs[hf][:, :])
                for x in bks + bvs:
                    ps.put(x)

            def kv_back(t):
                o = (t % 2) * 1024
                bT = ps.get()
                bTb = bT.bitcast(BF16)
                for h in range(8):
                    sch.tr(bTb[:, h * 128:(h + 1) * 128], tb[:, o + h * 128:o + (h + 1) * 128], ident[:])
                sch.ts("dve", KTn[:, :, tok0 + t * 128:tok0 + (t + 1) * 128],
                       bTb.rearrange("p (h n) -> p h n", n=128), gkn, ALU.mult)
                ps.put(bT)

            conv_pair(wA2b, 0, 2)
            kv_front(0)
            conv_pair(wA2b, 1, 3)
            W.done(wA2b)
            if has_next:
                norm_stats()
            kv_front(1)
            kv_back(0)
            if has_next:
                norm_tile(0)
                norm_tile(1)
            kv_front(2)
            kv_back(1)
            if has_next:
                norm_tile(2)
                norm_tile(3)
                if b + 2 < NB:
                    load_x(s, b + 2)
            kv_front(3)
            kv_back(2)
            kv_back(3)
            W.done(wA3)
            for t in range(4):
                sch.ts("dve", zk[:, t, :], ssk[:, t, :], sskpe[:, t:t + 1], ALU.add, QKH * EPS, ALU.add)
            sch.rsqrt_act(rk[:, b * 4:(b + 1) * 4, :], zk, 1.0, 0.0)

        def phaseBC(s, b):
            tok0 = b * QB
            sch.dma(out=tabs[:], in_=sTQ[:, b * 4:(b + 1) * 4, :, :], key="tab", writes=[tabs[:]])
            wQ2 = W.get()
            tb3 = tb[:, 0:1536].rearrange("p (h d) -> p h d", d=QKH)
            xnf = xn[:, :, :].rearrange("p a n -> p (a n)").bitcast(F32)
            t1 = xnf[:, 0:512].rearrange("p (h d) -> p h d", d=64)
            t2 = xnf[:, 512:1024].rearrange("p (h d) -> p h d", d=64)
            qb = {}
            qsets = [(0, 1, 2), (3, 4, 5)]

            conv_ops = [(j, i) for j in range(CW) for i in range(4)]
            conv_pos = [0]

            def conv_emit(nmax):
                while nmax > 0 and conv_pos[0] < len(conv_ops):
                    j, i = conv_ops[conv_pos[0]]
                    conv_pos[0] += 1
                    nmax -= 1
                    src = uT[:, i, tok0 + j + 1:tok0 + j + 1 + QB]
                    if j == 0:
                        sch.ts("dve", cacc[:, i, :], src, wdc[:, i, 0:1], ALU.mult, bdw[:, i:i + 1], ALU.add)
                    else:
                        sch.stt(cacc[:, i, :], src, wdc[:, i, j:j + 1], cacc[:, i, :], ALU.mult, ALU.add)

            def q_stage1(t):
                idx = qsets[t % 2]
                bq = [ps.take(pbank[i]) for i in idx]
                qv = psum[:, idx[0]:idx[0] + 3, :].rearrange("p b n -> p (b n)")
                qb[t] = (bq, qv)
                for j in range(3):
                    for c in range(2):
                        sch.mm(bq[j][:, :], qlT_all[:, c, tok0 + t * 128:tok0 + (t + 1) * 128],
                               wQ2[1][:, c * 1536 + j * 512:c * 1536 + (j + 1) * 512], c == 0, c == 1)
                for h in range(8):
                    jo = (h % 5) * QKH
                    sch.act(junk[:, jo:jo + QKH], qv[:, h * QKH:(h + 1) * QKH], AF.Square, accum=ssq[:, t, h:h + 1])
                sch.rsqrt_act(rq[:, t, :], ssq[:, t, :], 1.0 / QKH, EPS)

            def q_stage2(t):
                bq, qv = qb.pop(t)
                for h in range(8):
                    if h % 2 == 0:
                        sch.ts("dve", tb[:, h * QKH:(h + 1) * QKH], qv[:, h * QKH:(h + 1) * QKH], rq[:, t, h:h + 1], ALU.mult)
                    else:
                        sch.act(tb[:, h * QKH:(h + 1) * QKH], qv[:, h * QKH:(h + 1) * QKH], AF.Copy, scale=rq[:, t, h:h + 1])
                for x in bq:
                    ps.put(x)
                cosb = tabs[:, t, 0:1, :].broadcast_to([128, 8, 64])
                sch.tt("dve", t1, tb3[:, :, 128:192], cosb, ALU.mult)
                sch.tt("dve", t2[:, :, 0:32], tb3[:, :, 160:192], tabs[:, t, 1:2, 0:32].broadcast_to([128, 8, 32]), ALU.mult)
                sch.tt("dve", t2[:, :, 32:64], tb3[:, :, 128:160], tabs[:, t, 1:2, 32:64].broadcast_to([128, 8, 32]), ALU.mult)
                sch.tt("dve", qr[:, :, :], t1, t2, ALU.add)
                bT = ps.take(pbank[6])
                bTb = bT.bitcast(BF16)
                for h in range(8):
                    sch.tr(bTb[:, h * 128:(h + 1) * 128], tb3[:, h, 0:128], ident[:])
                bT2 = ps.take(pbank[7])
                bT2b = bT2.bitcast(BF16)
                qr2 = qr[:, :, :].rearrange("p h d -> p (h d)")
                for p in range(4):
                    sch.tr(bT2b[:, p * 128:(p + 1) * 128], qr2[:, p * 128:(p + 1) * 128], ident[:])
                conv_emit(4)
                sch.ts("dve", QTn[:, :, t * 128:(t + 1) * 128], bTb.rearrange("p (h n) -> p h n", n=128), gqn, ALU.mult)
                ps.put(bT)
                sch.copy("act", QTr[:, :, t * 128:(t + 1) * 128], bT2b[:, 0:512].rearrange("p (h n) -> p h n", n=128))
                ps.put(bT2)

            q_stage1(0)
            for t in range(4):
                if t + 1 < 4:
                    q_stage1(t + 1)
                q_stage2(t)
            W.done(wQ2)
            dbg("QTn", QTn)
            dbg("QTr", QTr)

            steps = [(h, kt) for h in range(H) for kt in range(NT)]
            LA = 5
            acc = {}
            pslots = {}

            def qk(i):
                h, kt = steps[i]
                bs = ps.get()
                sch.mm(bs[:, :], KTn[:, h, kt * 128:(kt + 1) * 128], QTn[:, h, :], True, False)
                sch.mm(bs[:, :], KTr[:, h % 2, kt * 128:(kt + 1) * 128], QTr[:, h // 2, :], False, True)
                p = pring.get()
                pslots[i] = p
                sch.act(p, bs[:, :], AF.Exp, scale=rk[:, kt, h:h + 1])
                ps.put(bs)

            quad = {}
            qcnt = [0]

            def pv(i):
                h, kt = steps[i]
                if kt == 0:
                    acc[h] = (ps.get(), ps.get())
                ao, asum = acc[h]
                p = pslots.pop(i)
                sch.mm(ao[:, :], Vt[:, kt, h * 128:(h + 1) * 128], p, kt == 0, kt == NT - 1)
                if QUAD:
                    quad.setdefault(h, []).append(p)
                    if kt % 4 == 3:
                        p0, p1, p2, p3 = quad.pop(h)
                        sa = psm[:, (qcnt[0] % 2) * 2, :]
                        sb_ = psm[:, (qcnt[0] % 2) * 2 + 1, :]
                        qcnt[0] += 1
                        sch.tt(QUAD, sa, p0, p1, ALU.add)
                        sch.tt(QUAD, sb_, p2, p3, ALU.add)
                        sch.tt(QUAD, sa, sa, sb_, ALU.add)
                        sch.mm(asum[:, :], ones[:], sa, kt == 3, kt == NT - 1)
                        for x in (p0, p1, p2, p3):
                            pring.put(x)
                else:
                    sch.mm(asum[:, :], ones[:], p, kt == 0, kt == NT - 1)
                    pring.put(p)
                if kt % 4 == 3:
                    conv_emit(5)
                if kt == NT - 1:
                    sch.recip(rc, asum[:, :], on_dve=(h >= H - 2))
                    sch.tt("dve", attnT[:, h, :], ao[:, :], rc, ALU.mult)
                    ps.put(ao)
                    ps.put(asum)
                    del acc[h]

            def ln_part1(t):
                i2 = t % 2
                bk = ps.get()
                for i in range(4):
                    sch.tr(bk[:, i * 128:(i + 1) * 128], cacc[:, i, t * 128:(t + 1) * 128], identf[:])
                sch.act(lnT[:, i2, :], bk[:, :], AF.Identity, accum=lsum[:, t:t + 1])
                sch.act(lnT[:, i2, :], bk[:, :], AF.Square, accum=lssq[:, t:t + 1])
                c = slice(t, t + 1)
                sch.ts("dve", lmean[:, c], lsum[:, c], 1.0 / CD, ALU.mult)
                sch.tt("dve", lmsq[:, c], lmean[:, c], lmean[:, c], ALU.mult)
                sch.ts("dve", lvar[:, c], lssq[:, c], 1.0 / CD, ALU.mult, EPS, ALU.add)
                sch.tt("dve", lvar[:, c], lvar[:, c], lmsq[:, c], ALU.subtract)
                sch.rsqrt_act(lrs[:, c], lvar[:, c], 1.0, 0.0)
                sch.ts("dve", lnT[:, i2, :], bk[:, :], lmean[:, c], ALU.subtract, lrs[:, c], ALU.mult)
                ps.put(bk)

            def ln_part2(t):
                i2 = t % 2
                bT = ps.get()
                bTb = bT.bitcast(BF16)
                for i in range(4):
                    sch.tr(bTb[:, i * 128:(i + 1) * 128], lnT[:, i2, i * 128:(i + 1) * 128], ident[:])
                for i in range(4):
                    sch.act(uact[:, i, t * 128:(t + 1) * 128], bTb[:, i * 128:(i + 1) * 128], AF.Silu,
                            scale=gln[:, i:i + 1], bias=bln[:, i:i + 1])
                ps.put(bT)

            ln_at = {}
            for t in range(4):
                p1 = (H - 2) * NT + LA + 1 + t * (NT // 2)
                ln_at.setdefault(p1, []).append((1, t))
                ln_at.setdefault(p1 + 7, []).append((2, t))
            ln_done = set()

            def ln_run(part, t):
                if (part, t) in ln_done:
                    return
                ln_done.add((part, t))
                if part == 1:
                    if t == 0:
                        conv_emit(10 ** 6)
                    ln_part1(t)
                else:
                    ln_part2(t)
            n = len(steps)
            for i in range(n + LA):
                if i < n:
                    qk(i)
                if i - LA >= 0:
                    pv(i - LA)
                if i == min(LA + 2, n + LA - 1):
                    load_x(s, b)
                if i == min(NT // 2 + LA, n + LA - 1):
                    norm_stats()
                for t in range(4):
                    if i == min(NT + LA + 2 * t, n + LA - 1):
                        norm_tile(t)
                for part, t in ln_at.get(i, ()):
                    ln_run(part, t)
            for t in range(4):
                ln_run(1, t)
                ln_run(2, t)
            dbg("attnT", attnT)
            dbg("cacc", cacc[:])
            dbg("uact", uact)
            for c in range(8):
                wM = W.get()
                bga, bgc, bao, bco = ps.get(), ps.get(), ps.get(), ps.get()
                for k in range(8):
                    sch.mm(bga[:, :], wM[1][:, (12 + k) * 128:(13 + k) * 128], xT[:, k, :], k == 0, k == 7)
                for k in range(8):
                    sch.mm(bgc[:, :], wM[1][:, (20 + k) * 128:(21 + k) * 128], xT[:, k, :], k == 0, k == 7)
                for k in range(8):
                    sch.mm(bao[:, :], wM[1][:, k * 128:(k + 1) * 128], attnT[:, k, :], k == 0, k == 7)
                for k in range(4):
                    sch.mm(bco[:, :], wM[1][:, (8 + k) * 128:(9 + k) * 128], uact[:, k, :], k == 0, k == 3)
                W.done(wM)
                sch.act(gsb[:, 0, :], bga[:, :], AF.Sigmoid)
                sch.act(gsb[:, 1, :], bgc[:, :], AF.Sigmoid)
                sch.tt("dve", mtmp[:, 0, :], bao[:, :], gsb[:, 0, :], ALU.mult)
                sch.tt("dve", mtmp[:, 1, :], bco[:, :], gsb[:, 1, :], ALU.mult)
                sch.tt("pool", mT[:, c, :], mtmp[:, 0, :], mtmp[:, 1, :], ALU.add)
                for x in (bga, bgc, bao, bco):
                    ps.put(x)
            dbg("mT", mT)
            wO = [W.get(), W.get()]
            for t in range(4):
                for hf in range(2):
                    bk = ps.get()
                    for k in range(8):
                        sch.mm(bk[:, :], mT[:, k, t * 128:(t + 1) * 128], wO[hf][1][:, k * 512:(k + 1) * 512], k == 0, k == 7)
                    sch.tt("dve", x1buf[:, t, hf * 512:(hf + 1) * 512], bk[:, :], x1buf[:, t, hf * 512:(hf + 1) * 512], ALU.add)
                    ps.put(bk)
                norm_stats_tile(t)
                if t >= 2:
                    norm_tile(t - 2)
            norm_tile(2)
            norm_tile(3)
            W.done(wO[0])
            W.done(wO[1])
            dbg("x1", x1buf[:])
            for u in range(8):
                wF = W.get()
                for m in range(4):
                    bk = ps.get()
                    for k in range(8):
                        sch.mm(bk[:, :], wF[1][:, (m * 8 + k) * 128:(m * 8 + k + 1) * 128], xT[:, k, :], k == 0, k == 7)
                    j = m % 2
                    sch.act(mtmp[:, j, :], bk[:, :], AF.Relu)
                    ps.put(bk)
                    sch.tt("pool", hT[:, u * 4 + m, :], mtmp[:, j, :], mtmp[:, j, :], ALU.mult)
                W.done(wF)
            yb = [ps.get() for _ in range(8)]
            for u in range(8):
                wF = W.get()
                for kk in range(4):
                    k = u * 4 + kk
                    for t in range(4):
                        for hf in range(2):
                            sch.mm(yb[t * 2 + hf][:, :], hT[:, k, t * 128:(t + 1) * 128],
                                   wF[1][:, kk * 1024 + hf * 512:kk * 1024 + (hf + 1) * 512], k == 0, k == 31)
                W.done(wF)
            for t in range(4):
                for hf in range(2):
                    sch.tt("dve", x1buf[:, t, hf * 512:(hf + 1) * 512], yb[t * 2 + hf][:, :],
                           x1buf[:, t, hf * 512:(hf + 1) * 512], ALU.add)
                    ps.put(yb[t * 2 + hf])
                r0 = tok0 + t * 128
                sch.dma(out=y_d[s, r0:r0 + 128, :], in_=x1buf[:, t, :], key=("y", t), reads=[x1buf[:, t, :]])

        if stop in ("prepass", "setup"):
            load_x(0, 0)
            for t in range(4):
                sch.dma(out=y_d[0, t * 128:(t + 1) * 128, :], in_=x1buf[:, t, :], key=("y", t), reads=[x1buf[:, t, :]])
        if stop == "normT":
            load_x(0, 0)
            norm_T()
            dbg("xT", xT[:])
        for s in range(nseq if stop is None or stop[0] in "AB" else 0):
            for b in range(NB):
                phaseA(s, b)
                if s == 0 and b == NB - 1:
                    dbg("KTn", KTn[:])
                    dbg("KTr", KTr[:])
                    dbg("Vt", Vt[:])
                    dbg("uT", uT[:])
                    dbg("rk", rk[:])
            for b in range(NB if (stop is None or stop[0] != "A") else 0):
                phaseBC(s, b)
        sch.final_wait()
        sch.emit()
        nops = sch.nops
    return nc, dbg_out, nops


def rope_tables(S):
    inv_freq = (1.0 / (np.float32(10000.0) ** (np.arange(0, 64, 2, dtype=np.float32) / np.float32(64)))).astype(np.float32)
    pos = np.arange(S, dtype=np.float32)
    ang = pos[:, None] * inv_freq[None, :]
    ang = np.concatenate([ang, ang], axis=-1).astype(np.float32)
    cos = np.cos(ang).astype(np.float32)
    sin = np.sin(ang).astype(np.float32)
    sin_s = sin.copy()
    sin_s[:, 0:32] = -sin_s[:, 0:32]
    return cos, sin_s


WNAMES = ["g_mix", "w_in", "g_q_lat", "w_uq", "g_kv_lat", "w_ukv", "g_qk_q", "g_qk_k", "w_o_attn", "w_dw", "b_dw",
          "g_conv_ln", "b_conv_ln", "w_o_conv", "w_out", "g_ffn", "w_ff1", "w_ff2"]


def weight_map(inputs):
    m = {}
    for n in WNAMES:
        a = np.asarray(inputs[n], dtype=np.float32)
        a = a[0]
        if n == "w_dw":
            a = a.reshape(CW, CD)
        m[n] = np.ascontiguousarray(a)
    return m


_PROG = {}


def kernel(**inputs):
    xp = np.asarray(inputs["x_prompt"], dtype=np.float32)
    xs = np.asarray(inputs["x_sample"], dtype=np.float32)
    S = xp.shape[1]
    nb_p, nb_s = xp.shape[0], xs.shape[0]
    xall = np.concatenate([xp, xs], axis=0)
    ntot = xall.shape[0]
    nseq = ntot // NCORES
    key = (nseq, S)
    if key not in _PROG:
        _PROG[key] = build_program(nseq, S)[0]
    nc = _PROG[key]
    wm = weight_map(inputs)
    cos, sin_s = rope_tables(S)
    in_maps = []
    for c in range(NCORES):
        m = dict(wm)
        m["x"] = np.ascontiguousarray(xall[c * nseq:(c + 1) * nseq])
        m["rope_cos"] = cos
        m["rope_sin"] = sin_s
        in_maps.append(m)
    res = run_bass_kernel_spmd(nc, in_maps, core_ids=list(range(NCORES)))
    yall = np.concatenate([r["y"] for r in res.results], axis=0)
    return (np.ascontiguousarray(yall[:nb_p]), np.ascontiguousarray(yall[nb_p:]))
```

```python
import numpy as np
from contextlib import ExitStack
import concourse.bass as bass
import concourse.mybir as mybir
from concourse.bass_utils import run_bass_kernel_spmd

F32 = mybir.dt.float32
BF16 = mybir.dt.bfloat16
AF = mybir.ActivationFunctionType
ALU = mybir.AluOpType

D = 1024
H = 8
QKH = 192
QL = 256
KVL = 256
CD = 512
CW = 31
DFF = 4096
EPS = 1e-6
INC = 3648
QB = 512
NCORES = 8
RING = 3
RECIP_ACT = True
QUAD = None


def _dsize(dt):
    return mybir.dt.size(dt)


class Op:
    __slots__ = ("eng", "fn", "waits", "sig", "sigval", "dma_key", "dma_val")


class Sched:
    ENGS = ("pe", "act", "dve", "pool", "sp")

    def __init__(self, nc, es, tag):
        self.nc = nc
        self.es = es
        self.tag = tag
        self.streams = {e: [] for e in self.ENGS}
        self.recs = {}
        self.dma_cnt = {}
        self.nops = 0

    @staticmethod
    def region(ap):
        name = ap.tensor.name
        sp = str(ap.space)
        if "DRAM" in sp.upper() or "HBM" in sp.upper():
            return None
        apl = ap.ap
        size = _dsize(ap.dtype)
        pstep, npart = apl[0]
        p0 = ap.offset // pstep
        f0 = ap.offset % pstep
        ext = 1 + sum((c - 1) * abs(s) for s, c in apl[1:])
        lo = f0 * size
        hi = (f0 + ext) * size
        if "PS" in sp.upper():
            lo = lo // 2048 * 2048
            hi = -(-hi // 2048) * 2048
            return (name, lo, hi, 0, 128)
        return (name, lo, hi, p0, p0 + npart)

    def _scan(self, reg, is_write, deps, eng=None):
        name, lo, hi, p0, p1 = reg
        rr = name == "psum"
        for r in self.recs.get(name, ()):
            if r[0] < hi and lo < r[1] and r[2] < p1 and p0 < r[3]:
                if is_write or r[5] or (rr and r[4].eng != eng):
                    kind = "raw" if (r[5] and not is_write) else "other"
                    prev = deps.get(r[4])
                    if prev is None or kind == "raw":
                        deps[r[4]] = kind

    def _record(self, reg, op, is_write):
        name, lo, hi, p0, p1 = reg
        lst = self.recs.setdefault(name, [])
        if is_write:
            lst[:] = [r for r in lst if not (lo <= r[0] and r[1] <= hi and p0 <= r[2] and r[3] <= p1)]
        else:
            lst[:] = [r for r in lst if not (not r[5] and r[4].eng == op.eng and r[4].dma_key is None
                                             and op.dma_key is None
                                             and r[0] == lo and r[1] == hi and r[2] == p0 and r[3] == p1)]
        lst.append((lo, hi, p0, p1, op, is_write))

    def add(self, eng, fn, reads=(), writes=(), dma_key=None):
        op = Op()
        op.eng = eng
        op.fn = fn
        op.sig = False
        op.sigval = 0
        op.dma_key = dma_key
        op.dma_val = 0
        rregs = [g for g in (self.region(a) for a in reads) if g is not None]
        wregs = [g for g in (self.region(a) for a in writes) if g is not None]
        deps = {}
        for g in rregs:
            self._scan(g, False, deps, eng)
        for g in wregs:
            self._scan(g, True, deps, eng)
        waits = []
        for p, kind in deps.items():
            if p.dma_key is not None:
                waits.append(("dma", p.dma_key, 16 * self.dma_cnt[p.dma_key]))
            else:
                if p.eng == eng and eng == "pe":
                    continue
                p.sig = True
                waits.append(("eng", p))
        op.waits = waits
        for g in rregs:
            self._record(g, op, False)
        for g in wregs:
            self._record(g, op, True)
        if dma_key is not None:
            self.dma_cnt[dma_key] = self.dma_cnt.get(dma_key, 0) + 1
            op.dma_val = 16 * self.dma_cnt[dma_key]
        self.streams[eng].append(op)
        self.nops += 1
        return op

    def mm(self, out, lhsT, rhs, start, stop):
        self.add("pe", lambda e: e.matmul(out, lhsT=lhsT, rhs=rhs, start=start, stop=stop),
                 reads=[lhsT, rhs], writes=[out])

    def tr(self, out, in_, ident):
        self.add("pe", lambda e: e.transpose(out=out, in_=in_, identity=ident),
                 reads=[in_, ident], writes=[out])

    def act(self, out, in_, func, scale=1.0, bias=0.0, accum=None):
        reads = [in_]
        if not isinstance(scale, (int, float)):
            reads.append(scale)
        if not isinstance(bias, (int, float)):
            reads.append(bias)
        writes = [out] + ([accum] if accum is not None else [])
        kw = {}
        if accum is not None:
            kw["accum_out"] = accum
        self.add("act", lambda e: e.activation(out=out, in_=in_, func=func, scale=scale, bias=bias, **kw),
                 reads=reads, writes=writes)

    def rsqrt_act(self, out, in_, scale, eps):
        self.act(out, in_, AF.Ln, scale=scale, bias=eps)
        self.act(out, out, AF.Exp, scale=-0.5)

    def ts(self, eng, out, in0, s1, op0, s2=None, op1=None):
        reads = [in0]
        if not isinstance(s1, (int, float)):
            reads.append(s1)
        if s2 is not None and not isinstance(s2, (int, float)):
            reads.append(s2)
        kw = {}
        if op1 is not None:
            kw["op1"] = op1
        self.add(eng, lambda e: e.tensor_scalar(out=out, in0=in0, scalar1=s1, scalar2=s2, op0=op0, **kw),
                 reads=reads, writes=[out])

    def tt(self, eng, out, in0, in1, op):
        self.add(eng, lambda e: e.tensor_tensor(out=out, in0=in0, in1=in1, op=op),
                 reads=[in0, in1], writes=[out])

    def stt(self, out, in0, scalar, in1, op0, op1):
        reads = [in0, in1]
        if not isinstance(scalar, (int, float)):
            reads.append(scalar)
        self.add("dve", lambda e: e.scalar_tensor_tensor(out=out, in0=in0, scalar=scalar, in1=in1, op0=op0, op1=op1),
                 reads=reads, writes=[out])

    def copy(self, eng, out, in_):
        if eng == "act":
            self.add("act", lambda e: e.activation(out=out, in_=in_, func=AF.Copy), reads=[in_], writes=[out])
        else:
            self.add(eng, lambda e: e.tensor_copy(out=out, in_=in_), reads=[in_], writes=[out])

    def recip(self, out, in_, on_dve=False):
        if RECIP_ACT and not on_dve:
            self.act(out, in_, AF.Ln)
            self.act(out, out, AF.Exp, scale=-1.0)
        else:
            self.add("dve", lambda e: e.reciprocal(out=out, in_=in_), reads=[in_], writes=[out])

    def memset(self, eng, out, val):
        self.add(eng, lambda e: e.memset(out, val), writes=[out])

    def dma(self, out, in_, key, reads=(), writes=(), slow=False):
        kw = {"allow_slow_non_contiguous": True} if slow else {}
        self.add("sp", lambda e: e.dma_start(out=out, in_=in_, **kw), reads=reads, writes=writes, dma_key=key)

    def final_wait(self):
        op = Op()
        op.eng = "sp"
        op.fn = None
        op.sig = False
        op.sigval = 0
        op.dma_key = None
        op.dma_val = 0
        op.waits = [("dma", k, 16 * c) for k, c in self.dma_cnt.items()]
        self.streams["sp"].append(op)

    def emit(self):
        nc = self.nc
        es = self.es
        esem = {e: es.enter_context(nc.semaphore(f"{self.tag}_e_{e}")) for e in self.ENGS if e != "sp"}
        dsem = {}
        for i, k in enumerate(self.dma_cnt):
            dsem[k] = es.enter_context(nc.semaphore(f"{self.tag}_d{i}"))
        for e, ops in self.streams.items():
            n = 0
            for op in ops:
                if op.dma_key is None and op.sig:
                    n += 1
                    op.sigval = n

        def run(ename, eng):
            waited = {}
            for op in self.streams[ename]:
                for w in op.waits:
                    if w[0] == "dma":
                        sem, val, sid = dsem[w[1]], w[2], ("d", w[1])
                    else:
                        p = w[1]
                        sem, val, sid = esem[p.eng], p.sigval, ("e", p.eng)
                    if val > waited.get(sid, 0):
                        eng.wait_ge(sem, val)
                        waited[sid] = val
                if op.fn is None:
                    continue
                inst = op.fn(eng)
                if op.dma_key is not None:
                    inst.then_inc(dsem[op.dma_key], 16)
                elif op.sig:
                    inst.then_inc(esem[ename], 1)

        with nc.Block() as block:
            @block.tensor
            def _(e):
                run("pe", e)

            @block.scalar
            def _(e):
                run("act", e)

            @block.vector
            def _(e):
                run("dve", e)

            @block.gpsimd
            def _(e):
                run("pool", e)

            @block.sync
            def _(e):
                run("sp", e)


class BufPool:
    def __init__(self, items):
        self.free = list(items)

    def get(self):
        assert self.free, "buffer pool exhausted (emission-order bug)"
        return self.free.pop(0)

    def put(self, x):
        self.free.append(x)

    def take(self, x):
        for i, f in enumerate(self.free):
            if f is x:
                return self.free.pop(i)
        raise AssertionError("requested buffer not free (emission-order bug)")


class WStream:
    def __init__(self, sch, ring, units):
        self.sch = sch
        self.ring = ring
        self.units = units
        self.R = ring.shape[1]
        self.freeslots = list(range(self.R))
        self.next_load = 0
        self.next_get = 0
        self.loaded = {}

    def _pump(self):
        while self.freeslots and self.next_load < len(self.units):
            s = self.freeslots.pop(0)
            src = self.units[self.next_load]
            n = src.shape[1]
            self.sch.dma(out=self.ring[:, s, 0:n], in_=src, key=("w", s), writes=[self.ring[:, s, :]])
            self.loaded[self.next_load] = s
            self.next_load += 1

    def get(self):
        self._pump()
        i = self.next_get
        self.next_get += 1
        assert i in self.loaded, "weight ring too small for simultaneous units"
        s = self.loaded[i]
        return (i, self.ring[:, s, :])

    def done(self, h):
        s = self.loaded.pop(h[0])
        self.freeslots.append(s)
        self._pump()


def build_program(nseq, S, debug=(), stop=None):
    NT = S // 128
    NB = S // QB
    assert S % QB == 0
    nc = bass.Bass("TRN2", target_bir_lowering=False)

    def din(name, shape):
        return nc.dram_tensor(name, list(shape), F32, kind="ExternalInput").ap()

    x_d = din("x", [nseq, S, D])
    g_mix = din("g_mix", [D])
    w_in = din("w_in", [D, INC])
    g_q_lat = din("g_q_lat", [QL])
    w_uq = din("w_uq", [QL, H * QKH])
    g_kv_lat = din("g_kv_lat", [KVL])
    w_ukv = din("w_ukv", [KVL, H * 256])
    g_qk_q = din("g_qk_q", [QKH])
    g_qk_k = din("g_qk_k", [QKH])
    w_o_attn = din("w_o_attn", [D, D])
    w_dw = din("w_dw", [CW, CD])
    b_dw = din("b_dw", [CD])
    g_conv_ln = din("g_conv_ln", [CD])
    b_conv_ln = din("b_conv_ln", [CD])
    w_o_conv = din("w_o_conv", [CD, D])
    w_out = din("w_out", [D, D])
    g_ffn = din("g_ffn", [D])
    w_ff1 = din("w_ff1", [D, DFF])
    w_ff2 = din("w_ff2", [DFF, D])
    cos_d = din("rope_cos", [S, 64])
    sin_d = din("rope_sin", [S, 64])
    y_d = nc.dram_tensor("y", [nseq, S, D], F32, kind="ExternalOutput").ap()

    def scr(name, shape, dt=BF16):
        return nc.dram_tensor(name, list(shape), dt, kind="Internal").ap()

    sA1 = scr("sA1", [128, 8 * 320])
    sA2 = scr("sA2", [2, 128, 4096])
    sA3 = scr("sA3", [128, 4096])
    sQ1 = scr("sQ1", [128, 2048])
    sQ2 = scr("sQ2", [128, 3072])
    sM = scr("sM", [8, 128, 3584])
    sO = scr("sO", [2, 128, 4096])
    sF1 = scr("sF1", [8, 128, 4096])
    sF2 = scr("sF2", [8, 128, 4096])
    sTK = scr("sTK", [128, NT, 2, 64], F32)
    sTQ = scr("sTQ", [128, NT, 2, 64], F32)

    dbg_out = {}

    with ExitStack() as es:
        E = es.enter_context
        NSTG = 6
        stg = E(nc.sbuf_tensor("stg", [128, NSTG, 4096], F32))
        stb = E(nc.sbuf_tensor("stb", [128, NSTG, 4096], BF16))
        gc = E(nc.sbuf_tensor("gc", [128, 20], F32))
        tcs = E(nc.sbuf_tensor("tcs", [128, 2, NT, 64], F32))
        gro = E(nc.sbuf_tensor("gro", [128, 4, 64], F32))
        tqk = E(nc.sbuf_tensor("tqk", [128, 2, NT, 2, 64], F32))
        sch = Sched(nc, es, "p")

        sch.dma(out=gc[:, 0:8], in_=g_mix.rearrange("(k p) -> p k", p=128), key="gc", writes=[gc[:]], slow=True)
        sch.dma(out=gc[:, 8:16], in_=g_ffn.rearrange("(k p) -> p k", p=128), key="gc", writes=[gc[:]], slow=True)
        sch.dma(out=gc[:, 16:18], in_=g_q_lat.rearrange("(k p) -> p k", p=128), key="gc", writes=[gc[:]], slow=True)
        sch.dma(out=gc[:, 18:20], in_=g_kv_lat.rearrange("(k p) -> p k", p=128), key="gc", writes=[gc[:]], slow=True)
        sch.dma(out=tcs[:, 0, :, :], in_=cos_d.rearrange("(n p) d -> p n d", p=128), key="tc0", writes=[tcs[:, 0, :, :]])
        sch.dma(out=tcs[:, 1, :, :], in_=sin_d.rearrange("(n p) d -> p n d", p=128), key="tc1", writes=[tcs[:, 1, :, :]])
        for qi, gvec in enumerate((g_qk_q, g_qk_k)):
            sch.dma(out=gro[:, 2 * qi, :], in_=gvec[128:192].partition_broadcast(128), key="gro", writes=[gro[:]])
            sch.dma(out=gro[:, 2 * qi + 1, 0:32], in_=gvec[160:192].partition_broadcast(128), key="gro", writes=[gro[:]])
            sch.dma(out=gro[:, 2 * qi + 1, 32:64], in_=gvec[128:160].partition_broadcast(128), key="gro", writes=[gro[:]])
        for qi, dst in enumerate((sTQ, sTK)):
            for cs in range(2):
                sch.tt("dve", tqk[:, qi, :, cs, :], tcs[:, cs, :, :],
                       gro[:, 2 * qi + cs:2 * qi + cs + 1, :].broadcast_to([128, NT, 64]), ALU.mult)
            sch.dma(out=dst, in_=tqk[:, qi, :, :, :], key=("tqk", qi), reads=[tqk[:, qi, :, :, :]])

        cnt = [0]

        def prep(src, ncols, scale, stores, src_view=None):
            i = cnt[0] % NSTG
            cnt[0] += 1
            dst = stg[:, i, 0:ncols] if src_view is None else src_view(stg[:, i, 0:ncols])
            sch.dma(out=dst, in_=src, key=("stg", i), writes=[stg[:, i, :]])
            if scale is not None:
                sch.ts("dve", stb[:, i, 0:ncols], stg[:, i, 0:ncols], scale, ALU.mult)
            else:
                sch.copy(("act", "dve", "act", "pool")[cnt[0] % 4], stb[:, i, 0:ncols], stg[:, i, 0:ncols])
            for sv, dv in stores(stb[:, i, 0:ncols]):
                sch.dma(out=dv, in_=sv, key=("stb", i), reads=[stb[:, i, :]])

        sA2v = sA2.rearrange("u p (j g k n) -> p u j g k n", j=2, g=2, k=8)
        sMv = sM.rearrange("c p (b n) -> p c b n", n=128)
        sF1v = sF1.rearrange("u p (k n) -> p u k n", k=8)
        for k in range(8):
            def st_win(v, k=k):
                out = [(v[:, 0:256], sQ1[:, k * 256:(k + 1) * 256]),
                       (v[:, 256:576], sA1[:, k * 320:(k + 1) * 320])]
                for g in range(2):
                    for u in range(2):
                        out.append((v[:, 576 + g * 512 + u * 256:576 + g * 512 + (u + 1) * 256].rearrange("p (j n) -> p j n", j=2),
                                    sA2v[:, u, :, g, k, :]))
                for g in range(2):
                    out.append((v[:, 1600 + g * 1024:1600 + (g + 1) * 1024].rearrange("p (c n) -> p c n", n=128),
                                sMv[:, :, 12 + 8 * g + k, :]))
                return out
            prep(w_in[k * 128:(k + 1) * 128, :], INC, gc[:, k:k + 1], st_win)
        for k in range(2):
            prep(w_uq[k * 128:(k + 1) * 128, :], 1536, gc[:, 16 + k:17 + k],
                 lambda v, k=k: [(v, sQ2[:, k * 1536:(k + 1) * 1536])])
        for k in range(2):
            def st_ukv(v, k=k):
                v4 = v.rearrange("p (h g n) -> p h g n", h=8, g=2)
                d4 = sA3[:, k * 2048:(k + 1) * 2048].rearrange("p (g h n) -> p g h n", g=2, h=8)
                return [(v4[:, :, g, :], d4[:, g, :, :]) for g in range(2)]
            prep(w_ukv[k * 128:(k + 1) * 128, :], 2048, gc[:, 18 + k:19 + k], st_ukv)
        for k in range(8):
            prep(w_o_attn[k * 128:(k + 1) * 128, :], 1024, None,
                 lambda v, k=k: [(v.rearrange("p (c n) -> p c n", n=128), sMv[:, :, k, :])])
        for k in range(4):
            prep(w_o_conv[k * 128:(k + 1) * 128, :], 1024, None,
                 lambda v, k=k: [(v.rearrange("p (c n) -> p c n", n=128), sMv[:, :, 8 + k, :])])
        sOv = sO.rearrange("h p (k n) -> p h k n", k=8)
        for k in range(8):
            prep(w_out[k * 128:(k + 1) * 128, :], 1024, None,
                 lambda v, k=k: [(v.rearrange("p (h n) -> p h n", h=2), sOv[:, :, k, :])])
        for k in range(8):
            prep(w_ff1[k * 128:(k + 1) * 128, :], 4096, gc[:, 8 + k:9 + k],
                 lambda v, k=k: [(v.rearrange("p (u n) -> p u n", u=8), sF1v[:, :, k, :])])
        for u in range(8):
            prep(w_ff2[u * 512:(u + 1) * 512, :].rearrange("(kk p) n -> p kk n", p=128), 4096, None,
                 lambda v, u=u: [(v, sF2[u])],
                 src_view=lambda a: a.rearrange("p (kk n) -> p kk n", kk=4))
        sch.final_wait()
        sch.emit()

    with ExitStack() as es:
        E = es.enter_context

        def sb(name, shape, dt):
            return E(nc.sbuf_tensor(name, list(shape), dt))

        KTn = sb("KTn", [128, 8, S], BF16)
        KTr = sb("KTr", [128, 2, S], BF16)
        Vt = sb("Vt", [128, NT, 1024], BF16)
        uT = sb("uT", [128, 4, S + 32], BF16)
        rk = sb("rk", [128, NT, 8], F32)
        wring = sb("wring", [128, RING, 4096], BF16)
        big = sb("big", [128, 16384], BF16)
        xT = sb("xT", [128, 8, QB], BF16)
        x1buf = sb("x1buf", [128, 4, 1024], F32)
        xn = sb("xn", [128, 2, 1024], BF16)
        latn = sb("latn", [128, 4, 256], BF16)
        latT = sb("latT", [128, 2, QB], BF16)
        tb = sb("tb", [128, 2048], BF16)
        qr = sb("qr", [128, 8, 64], BF16)
        krd = sb("krd", [128, 4, 128], BF16)
        tabs = sb("tabs", [128, 4, 2, 64], F32)
        gm = sb("gm", [128, 4, QB], F32)
        gsb = gm[:, 0:2, :]
        mtmp = gm[:, 2:4, :]
        cacc = gm
        psm = sb("psm", [128, 4, QB], BF16) if QUAD else None
        qlT_all = sb("qlT_all", [128, 2, S], BF16)
        latq = sb("latq", [128, 4, 256], BF16)
        rc = tb[:, 0:1024].bitcast(F32)
        lnT = sb("lnT", [128, 2, 512], BF16)
        junk = lnT[:, :, :].rearrange("p a n -> p (a n)")
        st = sb("st", [128, 256], F32)
        cst = sb("cst", [128, 64], F32)
        wdc = sb("wdc", [128, 4, CW], F32)
        ident = sb("ident", [128, 128], BF16)
        identf = sb("identf", [128, 128], F32)
        ones = sb("ones", [128, 128], BF16)
        psum = E(nc.psum_tensor("psum", [128, 8, 512], F32))

        hT = big[:, :].rearrange("p (k n) -> p k n", n=QB)
        QTn = big[:, 0:4096].rearrange("p (h n) -> p h n", n=QB)
        QTr = big[:, 4096:6144].rearrange("p (h n) -> p h n", n=QB)
        pring_v = big[:, 6144:10240].rearrange("p (h n) -> p h n", n=QB)
        attnT = big[:, 10240:14336].rearrange("p (h n) -> p h n", n=QB)
        uact = big[:, 14336:16384].rearrange("p (h n) -> p h n", n=QB)
        mT = QTn

        ss4 = st[:, 0:4]
        tmp4 = st[:, 4:8]
        rs4 = st[:, 8:12]
        ssQA = st[:, 216:224]
        tmpQA = st[:, 224:232]
        rsQA = st[:, 232:240]
        ssQ = ssQA[:, 0:4]
        ssA = ssQA[:, 4:8]
        rsQ = rsQA[:, 0:4]
        rsA = rsQA[:, 4:8]
        sskpe = st[:, 24:28]
        ssk = st[:, 32:64].rearrange("p (t h) -> p t h", h=8)
        zk = st[:, 64:96].rearrange("p (t h) -> p t h", h=8)
        ssq = st[:, 96:128].rearrange("p (t h) -> p t h", h=8)
        tmpq = st[:, 128:160].rearrange("p (t h) -> p t h", h=8)
        rq = st[:, 160:192].rearrange("p (t h) -> p t h", h=8)
        lsum = st[:, 192:196]
        lssq = st[:, 196:200]
        lmean = st[:, 200:204]
        lmsq = st[:, 204:208]
        lvar = st[:, 208:212]
        lrs = st[:, 212:216]
        mhalf = cst[:, 0:32]
        gqn = cst[:, 32:33]
        gkn = cst[:, 33:34]
        gln = cst[:, 36:40]
        bln = cst[:, 40:44]
        bdw = cst[:, 44:48]

        sch = Sched(nc, es, "m")
        pbank = [psum[:, i, :] for i in range(8)]
        ps = BufPool(pbank)
        pring = BufPool([pring_v[:, i, :] for i in range(8)])

        units = []
        for s in range(nseq):
            for b in range(NB):
                units += [sQ1, sA1, sA2[0], sA2[1], sA3]
            for b in range(NB):
                units += [sQ2] + [sM[c] for c in range(8)] + [sO[0], sO[1]]
                units += [sF1[u] for u in range(8)] + [sF2[u] for u in range(8)]
        W = WStream(sch, wring, units)

        def dbg(name, ap, dt=None):
            if name not in debug or name in dbg_out:
                return
            dt = dt or ap.dtype
            t = nc.dram_tensor("dbg_" + name, list(ap.shape), dt, kind="ExternalOutput").ap()
            dbg_out[name] = t
            sch.dma(out=t, in_=ap, key="dbg", reads=[ap])

        sch.memset("pool", ident[:], 0.0)
        sch.add("pool", lambda e: e.affine_select(out=ident[:], in_=ident[:], compare_op=ALU.not_equal, fill=1.0,
                                                  base=0, pattern=[[-1, 128]], channel_multiplier=1),
                reads=[ident[:]], writes=[ident[:]])
        sch.memset("pool", identf[:], 0.0)
        sch.add("pool", lambda e: e.affine_select(out=identf[:], in_=identf[:], compare_op=ALU.not_equal, fill=1.0,
                                                  base=0, pattern=[[-1, 128]], channel_multiplier=1),
                reads=[identf[:]], writes=[identf[:]])
        sch.memset("pool", ones[:], 1.0)
        sch.memset("pool", cst[:, 0:32], -0.5)
        sch.memset("pool", uT[:], 0.0)
        sch.memset("pool", KTr[:], 0.0)
        sch.dma(out=cst[:, 32:33], in_=g_qk_q[0:128].rearrange("(p o) -> p o", o=1), key="cst", writes=[cst[:, 32:64]], slow=True)
        sch.dma(out=cst[:, 33:34], in_=g_qk_k[0:128].rearrange("(p o) -> p o", o=1), key="cst", writes=[cst[:, 32:64]], slow=True)
        sch.dma(out=cst[:, 36:40], in_=g_conv_ln.rearrange("(i p) -> p i", p=128), key="cst", writes=[cst[:, 32:64]], slow=True)
        sch.dma(out=cst[:, 40:44], in_=b_conv_ln.rearrange("(i p) -> p i", p=128), key="cst", writes=[cst[:, 32:64]], slow=True)
        sch.dma(out=cst[:, 44:48], in_=b_dw.rearrange("(i p) -> p i", p=128), key="cst", writes=[cst[:, 32:64]], slow=True)
        wdr = mtmp[0:CW, 0, :]
        sch.dma(out=wdr, in_=w_dw, key="wdr", writes=[wdr])
        bk = ps.get()
        for i in range(4):
            sch.tr(bk[:, i * CW:(i + 1) * CW], mtmp[0:CW, 0, i * 128:(i + 1) * 128], identf[0:CW, 0:CW])
        sch.copy("dve", wdc[:], bk[:, 0:4 * CW].rearrange("p (i j) -> p i j", i=4))
        ps.put(bk)

        def load_x(s, b):
            for t in range(4):
                r0 = b * QB + t * 128
                sch.dma(out=x1buf[:, t, :], in_=x_d[s, r0:r0 + 128, :], key=("x", t), writes=[x1buf[:, t, :]])

        def norm_stats():
            for t in range(4):
                sch.act(junk[:, :], x1buf[:, t, :], AF.Square, accum=ss4[:, t:t + 1])
            sch.rsqrt_act(rs4, ss4, 1.0 / D, EPS)

        def norm_stats_tile(t):
            sch.act(junk[:, :], x1buf[:, t, :], AF.Square, accum=ss4[:, t:t + 1])
            sch.rsqrt_act(rs4[:, t:t + 1], ss4[:, t:t + 1], 1.0 / D, EPS)

        def norm_tile(t):
            i = t % 2
            sch.ts("dve", xn[:, i, :], x1buf[:, t, :], rs4[:, t:t + 1], ALU.mult)
            bT = ps.get()
            bTb = bT.bitcast(BF16)
            for k in range(8):
                sch.tr(bTb[:, k * 128:(k + 1) * 128], xn[:, i, k * 128:(k + 1) * 128], ident[:])
            sch.copy("act" if t % 2 else "dve", xT[:, :, t * 128:(t + 1) * 128],
                     bTb.rearrange("p (k n) -> p k n", n=128))
            ps.put(bT)

        def norm_T():
            norm_stats()
            for t in range(4):
                norm_tile(t)

        def phaseA(s, b):
            tok0 = b * QB
            sch.dma(out=tabs[:], in_=sTK[:, b * 4:(b + 1) * 4, :, :], key="tab", writes=[tabs[:]])
            has_next = b + 1 < NB
            if b == 0:
                load_x(s, 0)
                norm_T()
                if has_next:
                    load_x(s, 1)
            wQ1 = W.get()
            wA1 = W.get()
            banks = []
            bky = ps.get()
            for t in range(4):
                bk = ps.get()
                banks.append(bk)
                xs = xT[:, :, t * 128:(t + 1) * 128]
                for k in range(8):
                    sch.mm(bk[:, 0:256], xs[:, k, :], wQ1[1][:, k * 256:(k + 1) * 256], k == 0, k == 7)
                for k in range(8):
                    sch.mm(bk[:, 256:512], xs[:, k, :], wA1[1][:, k * 320:k * 320 + 256], k == 0, k == 7)
                for k in range(8):
                    sch.mm(bky[:, t * 64:(t + 1) * 64], xs[:, k, :], wA1[1][:, k * 320 + 256:(k + 1) * 320], k == 0, k == 7)
                sch.act(junk[:, 0:256], bk[:, 0:256], AF.Square, accum=ssQ[:, t:t + 1])
                sch.act(junk[:, 256:512], bk[:, 256:512], AF.Square, accum=ssA[:, t:t + 1])
                sch.act(junk[:, 512:576], bky[:, t * 64:(t + 1) * 64], AF.Square, accum=sskpe[:, t:t + 1])
            W.done(wQ1)
            W.done(wA1)
            sch.rsqrt_act(rsQA, ssQA, 1.0 / KVL, EPS)
            t1 = mtmp[:, 0, 0:256].rearrange("p (t d) -> p t d", d=64)
            t2 = mtmp[:, 0, 256:512].rearrange("p (t d) -> p t d", d=64)
            ky4 = bky[:, 0:256].rearrange("p (t d) -> p t d", d=64)
            sch.tt("dve", t1, ky4, tabs[:, :, 0, :], ALU.mult)
            sch.tt("dve", t2[:, :, 0:32], ky4[:, :, 32:64], tabs[:, :, 1, 0:32], ALU.mult)
            sch.tt("dve", t2[:, :, 32:64], ky4[:, :, 0:32], tabs[:, :, 1, 32:64], ALU.mult)
            ps.put(bky)
            krd4 = krd[:, :, :].rearrange("p t (r d) -> p t r d", r=2)
            sch.tt("dve", krd4[:, :, 0, :], t1, t2, ALU.add)
            sch.tt("dve", krd4[:, :, 1, :], t1, t2, ALU.add)
            for t in range(4):
                bk = banks[t]
                sch.ts("dve", latn[:, t, :], bk[:, 256:512], rsA[:, t:t + 1], ALU.mult)
                sch.ts("dve", latq[:, t, :], bk[:, 0:256], rsQ[:, t:t + 1], ALU.mult)
                ps.put(bk)

            def conv_pair(wA2, j, i):
                ba = ps.get()
                bg = ps.get()
                for k in range(8):
                    o = ((j * 2 + 1) * 8 + k) * 128
                    sch.mm(bg[:, :], wA2[1][:, o:o + 128], xT[:, k, :], k == 0, k == 7)
                for k in range(8):
                    o = ((j * 2 + 0) * 8 + k) * 128
                    sch.mm(ba[:, :], wA2[1][:, o:o + 128], xT[:, k, :], k == 0, k == 7)
                sch.act(gsb[:, j, :], bg[:, :], AF.Sigmoid)
                sch.tt("dve", uT[:, i, 16 + tok0:16 + tok0 + QB], ba[:, :], gsb[:, j, :], ALU.mult)
                ps.put(ba)
                ps.put(bg)

            def lat_T(t):
                bT = ps.get()
                bTb = bT.bitcast(BF16)
                sch.tr(bTb[:, 0:128], latn[:, t, 0:128], ident[:])
                sch.tr(bTb[:, 128:256], latn[:, t, 128:256], ident[:])
                sch.tr(bTb[:, 256:384], krd[:, t, :], ident[:])
                sch.tr(bTb[:, 384:512], latq[:, t, 0:128], ident[:])
                sch.tr(bTb[:, 512:640], latq[:, t, 128:256], ident[:])
                sch.copy("act", latT[:, :, t * 128:(t + 1) * 128], bTb[:, 0:256].rearrange("p (c n) -> p c n", n=128))
                sch.copy("act", KTr[0:64, 0, tok0 + t * 128:tok0 + (t + 1) * 128], bTb[0:64, 256:384])
                sch.copy("act", KTr[64:128, 1, tok0 + t * 128:tok0 + (t + 1) * 128], bTb[64:128, 256:384])
                sch.copy("dve", qlT_all[:, :, tok0 + t * 128:tok0 + (t + 1) * 128],
                         bTb[:, 384:640].rearrange("p (c n) -> p c n", n=128))
                ps.put(bT)

            wA2a = W.get()
            conv_pair(wA2a, 0, 0)
            lat_T(0)
            lat_T(1)
            conv_pair(wA2a, 1, 1)
            W.done(wA2a)
            lat_T(2)
            lat_T(3)
            wA2b = W.get()
            wA3 = W.get()

            def kv_front(t):
                tile = b * 4 + t
                bks = [ps.get(), ps.get()]
                bvs = [ps.get(), ps.get()]
                for hf in range(2):
                    for c in range(2):
                        sch.mm(bks[hf][:, :], latT[:, c, t * 128:(t + 1) * 128],
                               wA3[1][:, c * 2048 + hf * 512:c * 2048 + (hf + 1) * 512], c == 0, c == 1)
                for hf in range(2):
                    for c in range(2):
                        sch.mm(bvs[hf][:, :], latT[:, c, t * 128:(t + 1) * 128],
                               wA3[1][:, c * 2048 + 1024 + hf * 512:c * 2048 + 1024 + (hf + 1) * 512], c == 0, c == 1)
                for h in range(8):
                    sch.act(junk[:, h * 128:(h + 1) * 128], bks[h // 4][:, (h % 4) * 128:(h % 4 + 1) * 128], AF.Square,
                            accum=ssk[:, t, h:h + 1])
                o = (t % 2) * 1024
                for hf in range(2):
                    sch.copy("dve", tb[:, o + hf * 512:o + (hf + 1) * 512], bks[hf][:, :])
                    sch.copy("act", Vt[:, tile, hf * 512:(hf + 1) * 512], bvs[hf][:, :])
                for x in bks + bvs:
                    ps.put(x)

            def kv_back(t):
                o = (t % 2) * 1024
                bT = ps.get()
                bTb = bT.bitcast(BF16)
                for h in range(8):
                    sch.tr(bTb[:, h * 128:(h + 1) * 128], tb[:, o + h * 128:o + (h + 1) * 128], ident[:])
                sch.ts("dve", KTn[:, :, tok0 + t * 128:tok0 + (t + 1) * 128],
                       bTb.rearrange("p (h n) -> p h n", n=128), gkn, ALU.mult)
                ps.put(bT)

            conv_pair(wA2b, 0, 2)
            kv_front(0)
            conv_pair(wA2b, 1, 3)
            W.done(wA2b)
            if has_next:
                norm_stats()
            kv_front(1)
            kv_back(0)
            if has_next:
                norm_tile(0)
                norm_tile(1)
            kv_front(2)
            kv_back(1)
            if has_next:
                norm_tile(2)
                norm_tile(3)
                if b + 2 < NB:
                    load_x(s, b + 2)
            kv_front(3)
            kv_back(2)
            kv_back(3)
            W.done(wA3)
            for t in range(4):
                sch.ts("dve", zk[:, t, :], ssk[:, t, :], sskpe[:, t:t + 1], ALU.add, QKH * EPS, ALU.add)
            sch.rsqrt_act(rk[:, b * 4:(b + 1) * 4, :], zk, 1.0, 0.0)

        def phaseBC(s, b):
            tok0 = b * QB
            sch.dma(out=tabs[:], in_=sTQ[:, b * 4:(b + 1) * 4, :, :], key="tab", writes=[tabs[:]])
            wQ2 = W.get()
            tb3 = tb[:, 0:1536].rearrange("p (h d) -> p h d", d=QKH)
            xnf = xn[:, :, :].rearrange("p a n -> p (a n)").bitcast(F32)
            t1 = xnf[:, 0:512].rearrange("p (h d) -> p h d", d=64)
            t2 = xnf[:, 512:1024].rearrange("p (h d) -> p h d", d=64)
            qb = {}
            qsets = [(0, 1, 2), (3, 4, 5)]

            conv_ops = [(j, i) for j in range(CW) for i in range(4)]
            conv_pos = [0]

            def conv_emit(nmax):
                while nmax > 0 and conv_pos[0] < len(conv_ops):
                    j, i = conv_ops[conv_pos[0]]
                    conv_pos[0] += 1
                    nmax -= 1
                    src = uT[:, i, tok0 + j + 1:tok0 + j + 1 + QB]
                    if j == 0:
                        sch.ts("dve", cacc[:, i, :], src, wdc[:, i, 0:1], ALU.mult, bdw[:, i:i + 1], ALU.add)
                    else:
                        sch.stt(cacc[:, i, :], src, wdc[:, i, j:j + 1], cacc[:, i, :], ALU.mult, ALU.add)

            def q_stage1(t):
                idx = qsets[t % 2]
                bq = [ps.take(pbank[i]) for i in idx]
                qv = psum[:, idx[0]:idx[0] + 3, :].rearrange("p b n -> p (b n)")
                qb[t] = (bq, qv)
                for j in range(3):
                    for c in range(2):
                        sch.mm(bq[j][:, :], qlT_all[:, c, tok0 + t * 128:tok0 + (t + 1) * 128],
                               wQ2[1][:, c * 1536 + j * 512:c * 1536 + (j + 1) * 512], c == 0, c == 1)
                for h in range(8):
                    jo = (h % 5) * QKH
                    sch.act(junk[:, jo:jo + QKH], qv[:, h * QKH:(h + 1) * QKH], AF.Square, accum=ssq[:, t, h:h + 1])
                sch.rsqrt_act(rq[:, t, :], ssq[:, t, :], 1.0 / QKH, EPS)

            def q_stage2(t):
                bq, qv = qb.pop(t)
                for h in range(8):
                    if h % 2 == 0:
                        sch.ts("dve", tb[:, h * QKH:(h + 1) * QKH], qv[:, h * QKH:(h + 1) * QKH], rq[:, t, h:h + 1], ALU.mult)
                    else:
                        sch.act(tb[:, h * QKH:(h + 1) * QKH], qv[:, h * QKH:(h + 1) * QKH], AF.Copy, scale=rq[:, t, h:h + 1])
                for x in bq:
                    ps.put(x)
                cosb = tabs[:, t, 0:1, :].broadcast_to([128, 8, 64])
                sch.tt("dve", t1, tb3[:, :, 128:192], cosb, ALU.mult)
                sch.tt("dve", t2[:, :, 0:32], tb3[:, :, 160:192], tabs[:, t, 1:2, 0:32].broadcast_to([128, 8, 32]), ALU.mult)
                sch.tt("dve", t2[:, :, 32:64], tb3[:, :, 128:160], tabs[:, t, 1:2, 32:64].broadcast_to([128, 8, 32]), ALU.mult)
                sch.tt("dve", qr[:, :, :], t1, t2, ALU.add)
                bT = ps.take(pbank[6])
                bTb = bT.bitcast(BF16)
                for h in range(8):
                    sch.tr(bTb[:, h * 128:(h + 1) * 128], tb3[:, h, 0:128], ident[:])
                bT2 = ps.take(pbank[7])
                bT2b = bT2.bitcast(BF16)
                qr2 = qr[:, :, :].rearrange("p h d -> p (h d)")
                for p in range(4):
                    sch.tr(bT2b[:, p * 128:(p + 1) * 128], qr2[:, p * 128:(p + 1) * 128], ident[:])
                conv_emit(4)
                sch.ts("dve", QTn[:, :, t * 128:(t + 1) * 128], bTb.rearrange("p (h n) -> p h n", n=128), gqn, ALU.mult)
                ps.put(bT)
                sch.copy("act", QTr[:, :, t * 128:(t + 1) * 128], bT2b[:, 0:512].rearrange("p (h n) -> p h n", n=128))
                ps.put(bT2)

            q_stage1(0)
            for t in range(4):
                if t + 1 < 4:
                    q_stage1(t + 1)
                q_stage2(t)
            W.done(wQ2)
            dbg("QTn", QTn)
            dbg("QTr", QTr)

            steps = [(h, kt) for h in range(H) for kt in range(NT)]
            LA = 4
            acc = {}
            pslots = {}

            def qk(i):
                h, kt = steps[i]
                bs = ps.get()
                sch.mm(bs[:, :], KTn[:, h, kt * 128:(kt + 1) * 128], QTn[:, h, :], True, False)
                sch.mm(bs[:, :], KTr[:, h % 2, kt * 128:(kt + 1) * 128], QTr[:, h // 2, :], False, True)
                p = pring.get()
                pslots[i] = p
                sch.act(p, bs[:, :], AF.Exp, scale=rk[:, kt, h:h + 1])
                ps.put(bs)

            quad = {}
            qcnt = [0]

            def pv(i):
                h, kt = steps[i]
                if kt == 0:
                    acc[h] = (ps.get(), ps.get())
                ao, asum = acc[h]
                p = pslots.pop(i)
                sch.mm(ao[:, :], Vt[:, kt, h * 128:(h + 1) * 128], p, kt == 0, kt == NT - 1)
                if QUAD:
                    quad.setdefault(h, []).append(p)
                    if kt % 4 == 3:
                        p0, p1, p2, p3 = quad.pop(h)
                        sa = psm[:, (qcnt[0] % 2) * 2, :]
                        sb_ = psm[:, (qcnt[0] % 2) * 2 + 1, :]
                        qcnt[0] += 1
                        sch.tt(QUAD, sa, p0, p1, ALU.add)
                        sch.tt(QUAD, sb_, p2, p3, ALU.add)
                        sch.tt(QUAD, sa, sa, sb_, ALU.add)
                        sch.mm(asum[:, :], ones[:], sa, kt == 3, kt == NT - 1)
                        for x in (p0, p1, p2, p3):
                            pring.put(x)
                else:
                    sch.mm(asum[:, :], ones[:], p, kt == 0, kt == NT - 1)
                    pring.put(p)
                if kt % 4 == 3:
                    conv_emit(5)
                if kt == NT - 1:
                    sch.recip(rc, asum[:, :], on_dve=(h >= H - 2))
                    sch.tt("dve", attnT[:, h, :], ao[:, :], rc, ALU.mult)
                    ps.put(ao)
                    ps.put(asum)
                    del acc[h]

            def ln_part1(t):
                i2 = t % 2
                bk = ps.get()
                for i in range(4):
                    sch.tr(bk[:, i * 128:(i + 1) * 128], cacc[:, i, t * 128:(t + 1) * 128], identf[:])
                sch.act(lnT[:, i2, :], bk[:, :], AF.Identity, accum=lsum[:, t:t + 1])
                sch.act(lnT[:, i2, :], bk[:, :], AF.Square, accum=lssq[:, t:t + 1])
                c = slice(t, t + 1)
                sch.ts("dve", lmean[:, c], lsum[:, c], 1.0 / CD, ALU.mult)
                sch.tt("dve", lmsq[:, c], lmean[:, c], lmean[:, c], ALU.mult)
                sch.ts("dve", lvar[:, c], lssq[:, c], 1.0 / CD, ALU.mult, EPS, ALU.add)
                sch.tt("dve", lvar[:, c], lvar[:, c], lmsq[:, c], ALU.subtract)
                sch.rsqrt_act(lrs[:, c], lvar[:, c], 1.0, 0.0)
                sch.ts("dve", lnT[:, i2, :], bk[:, :], lmean[:, c], ALU.subtract, lrs[:, c], ALU.mult)
                ps.put(bk)

            def ln_part2(t):
                i2 = t % 2
                bT = ps.get()
                bTb = bT.bitcast(BF16)
                for i in range(4):
                    sch.tr(bTb[:, i * 128:(i + 1) * 128], lnT[:, i2, i * 128:(i + 1) * 128], ident[:])
                for i in range(4):
                    sch.act(uact[:, i, t * 128:(t + 1) * 128], bTb[:, i * 128:(i + 1) * 128], AF.Silu,
                            scale=gln[:, i:i + 1], bias=bln[:, i:i + 1])
                ps.put(bT)

            ln_at = {}
            for t in range(4):
                p1 = (H - 2) * NT + LA + 1 + t * (NT // 2)
                ln_at.setdefault(p1, []).append((1, t))
                ln_at.setdefault(p1 + 7, []).append((2, t))
            ln_done = set()

            def ln_run(part, t):
                if (part, t) in ln_done:
                    return
                ln_done.add((part, t))
                if part == 1:
                    if t == 0:
                        conv_emit(10 ** 6)
                    ln_part1(t)
                else:
                    ln_part2(t)
            n = len(steps)
            for i in range(n + LA):
                if i < n:
                    qk(i)
                if i - LA >= 0:
                    pv(i - LA)
                if i == min(LA + 2, n + LA - 1):
                    load_x(s, b)
                if i == min(NT // 2 + LA, n + LA - 1):
                    norm_stats()
                for t in range(4):
                    if i == min(NT + LA + 2 * t, n + LA - 1):
                        norm_tile(t)
                for part, t in ln_at.get(i, ()):
                    ln_run(part, t)
            for t in range(4):
                ln_run(1, t)
                ln_run(2, t)
            dbg("attnT", attnT)
            dbg("cacc", cacc[:])
            dbg("uact", uact)
            for c in range(8):
                wM = W.get()
                bga, bgc, bao, bco = ps.get(), ps.get(), ps.get(), ps.get()
                for k in range(8):
                    sch.mm(bga[:, :], wM[1][:, (12 + k) * 128:(13 + k) * 128], xT[:, k, :], k == 0, k == 7)
                for k in range(8):
                    sch.mm(bgc[:, :], wM[1][:, (20 + k) * 128:(21 + k) * 128], xT[:, k, :], k == 0, k == 7)
                for k in range(8):
                    sch.mm(bao[:, :], wM[1][:, k * 128:(k + 1) * 128], attnT[:, k, :], k == 0, k == 7)
                for k in range(4):
                    sch.mm(bco[:, :], wM[1][:, (8 + k) * 128:(9 + k) * 128], uact[:, k, :], k == 0, k == 3)
                W.done(wM)
                sch.act(gsb[:, 0, :], bga[:, :], AF.Sigmoid)
                sch.act(gsb[:, 1, :], bgc[:, :], AF.Sigmoid)
                sch.tt("dve", mtmp[:, 0, :], bao[:, :], gsb[:, 0, :], ALU.mult)
                sch.tt("dve", mtmp[:, 1, :], bco[:, :], gsb[:, 1, :], ALU.mult)
                sch.tt("pool", mT[:, c, :], mtmp[:, 0, :], mtmp[:, 1, :], ALU.add)
                for x in (bga, bgc, bao, bco):
                    ps.put(x)
            dbg("mT", mT)
            wO = [W.get(), W.get()]
            for t in range(4):
                for hf in range(2):
                    bk = ps.get()
                    for k in range(8):
                        sch.mm(bk[:, :], mT[:, k, t * 128:(t + 1) * 128], wO[hf][1][:, k * 512:(k + 1) * 512], k == 0, k == 7)
                    sch.tt("dve", x1buf[:, t, hf * 512:(hf + 1) * 512], bk[:, :], x1buf[:, t, hf * 512:(hf + 1) * 512], ALU.add)
                    ps.put(bk)
                norm_stats_tile(t)
                if t >= 2:
                    norm_tile(t - 2)
            norm_tile(2)
            norm_tile(3)
            W.done(wO[0])
            W.done(wO[1])
            dbg("x1", x1buf[:])
            for u in range(8):
                wF = W.get()
                for m in range(4):
                    bk = ps.get()
                    for k in range(8):
                        sch.mm(bk[:, :], wF[1][:, k * 512 + m * 128:k * 512 + (m + 1) * 128], xT[:, k, :], k == 0, k == 7)
                    j = m % 2
                    sch.act(mtmp[:, j, :], bk[:, :], AF.Relu)
                    ps.put(bk)
                    sch.tt("pool", hT[:, u * 4 + m, :], mtmp[:, j, :], mtmp[:, j, :], ALU.mult)
                W.done(wF)
            yb = [ps.get() for _ in range(8)]
            for u in range(8):
                wF = W.get()
                for kk in range(4):
                    k = u * 4 + kk
                    for t in range(4):
                        for hf in range(2):
                            sch.mm(yb[t * 2 + hf][:, :], hT[:, k, t * 128:(t + 1) * 128],
                                   wF[1][:, kk * 1024 + hf * 512:kk * 1024 + (hf + 1) * 512], k == 0, k == 31)
                W.done(wF)
            for t in range(4):
                for hf in range(2):
                    sch.tt("dve", x1buf[:, t, hf * 512:(hf + 1) * 512], yb[t * 2 + hf][:, :],
                           x1buf[:, t, hf * 512:(hf + 1) * 512], ALU.add)
                    ps.put(yb[t * 2 + hf])
                r0 = tok0 + t * 128
                sch.dma(out=y_d[s, r0:r0 + 128, :], in_=x1buf[:, t, :], key=("y", t), reads=[x1buf[:, t, :]])

        if stop in ("prepass", "setup"):
            load_x(0, 0)
            for t in range(4):
                sch.dma(out=y_d[0, t * 128:(t + 1) * 128, :], in_=x1buf[:, t, :], key=("y", t), reads=[x1buf[:, t, :]])
        if stop == "normT":
            load_x(0, 0)
            norm_T()
            dbg("xT", xT[:])
        for s in range(nseq if stop is None or stop[0] in "AB" else 0):
            for b in range(NB):
                phaseA(s, b)
                if s == 0 and b == NB - 1:
                    dbg("KTn", KTn[:])
                    dbg("KTr", KTr[:])
                    dbg("Vt", Vt[:])
                    dbg("uT", uT[:])
                    dbg("rk", rk[:])
            for b in range(NB if (stop is None or stop[0] != "A") else 0):
                phaseBC(s, b)
        sch.final_wait()
        sch.emit()
        nops = sch.nops
    return nc, dbg_out, nops


def rope_tables(S):
    inv_freq = (1.0 / (np.float32(10000.0) ** (np.arange(0, 64, 2, dtype=np.float32) / np.float32(64)))).astype(np.float32)
    pos = np.arange(S, dtype=np.float32)
    ang = pos[:, None] * inv_freq[None, :]
    ang = np.concatenate([ang, ang], axis=-1).astype(np.float32)
    cos = np.cos(ang).astype(np.float32)
    sin = np.sin(ang).astype(np.float32)
    sin_s = sin.copy()
    sin_s[:, 0:32] = -sin_s[:, 0:32]
    return cos, sin_s


WNAMES = ["g_mix", "w_in", "g_q_lat", "w_uq", "g_kv_lat", "w_ukv", "g_qk_q", "g_qk_k", "w_o_attn", "w_dw", "b_dw",
          "g_conv_ln", "b_conv_ln", "w_o_conv", "w_out", "g_ffn", "w_ff1", "w_ff2"]


def weight_map(inputs):
    m = {}
    for n in WNAMES:
        a = np.asarray(inputs[n], dtype=np.float32)
        a = a[0]
        if n == "w_dw":
            a = a.reshape(CW, CD)
        m[n] = np.ascontiguousarray(a)
    return m


_PROG = {}


def kernel(**inputs):
    xp = np.asarray(inputs["x_prompt"], dtype=np.float32)
    xs = np.asarray(inputs["x_sample"], dtype=np.float32)
    S = xp.shape[1]
    nb_p, nb_s = xp.shape[0], xs.shape[0]
    xall = np.concatenate([xp, xs], axis=0)
    ntot = xall.shape[0]
    nseq = ntot // NCORES
    key = (nseq, S)
    if key not in _PROG:
        _PROG[key] = build_program(nseq, S)[0]
    nc = _PROG[key]
    wm = weight_map(inputs)
    cos, sin_s = rope_tables(S)
    in_maps = []
    for c in range(NCORES):
        m = dict(wm)
        m["x"] = np.ascontiguousarray(xall[c * nseq:(c + 1) * nseq])
        m["rope_cos"] = cos
        m["rope_sin"] = sin_s
        in_maps.append(m)
    res = run_bass_kernel_spmd(nc, in_maps, core_ids=list(range(NCORES)))
    yall = np.concatenate([r["y"] for r in res.results], axis=0)
    return (np.ascontiguousarray(yall[:nb_p]), np.ascontiguousarray(yall[nb_p:]))
```

```python
import numpy as np
from contextlib import ExitStack
import concourse.bass as bass
import concourse.mybir as mybir
from concourse.bass_utils import run_bass_kernel_spmd

F32 = mybir.dt.float32
BF16 = mybir.dt.bfloat16
AF = mybir.ActivationFunctionType
ALU = mybir.AluOpType

D = 1024
H = 8
QKH = 192
QL = 256
KVL = 256
CD = 512
CW = 31
DFF = 4096
EPS = 1e-6
INC = 3648
QB = 512
NCORES = 8
RING = 3
RECIP_ACT = True
QUAD = None


def _dsize(dt):
    return mybir.dt.size(dt)


class Op:
    __slots__ = ("eng", "fn", "waits", "sig", "sigval", "dma_key", "dma_val")


class Sched:
    ENGS = ("pe", "act", "dve", "pool", "sp")

    def __init__(self, nc, es, tag):
        self.nc = nc
        self.es = es
        self.tag = tag
        self.streams = {e: [] for e in self.ENGS}
        self.recs = {}
        self.dma_cnt = {}
        self.nops = 0

    @staticmethod
    def region(ap):
        name = ap.tensor.name
        sp = str(ap.space)
        if "DRAM" in sp.upper() or "HBM" in sp.upper():
            return None
        apl = ap.ap
        size = _dsize(ap.dtype)
        pstep, npart = apl[0]
        p0 = ap.offset // pstep
        f0 = ap.offset % pstep
        ext = 1 + sum((c - 1) * abs(s) for s, c in apl[1:])
        lo = f0 * size
        hi = (f0 + ext) * size
        if "PS" in sp.upper():
            lo = lo // 2048 * 2048
            hi = -(-hi // 2048) * 2048
            return (name, lo, hi, 0, 128)
        return (name, lo, hi, p0, p0 + npart)

    def _scan(self, reg, is_write, deps, eng=None):
        name, lo, hi, p0, p1 = reg
        rr = name == "psum"
        for r in self.recs.get(name, ()):
            if r[0] < hi and lo < r[1] and r[2] < p1 and p0 < r[3]:
                if is_write or r[5] or (rr and r[4].eng != eng):
                    kind = "raw" if (r[5] and not is_write) else "other"
                    prev = deps.get(r[4])
                    if prev is None or kind == "raw":
                        deps[r[4]] = kind

    def _record(self, reg, op, is_write):
        name, lo, hi, p0, p1 = reg
        lst = self.recs.setdefault(name, [])
        if is_write:
            lst[:] = [r for r in lst if not (lo <= r[0] and r[1] <= hi and p0 <= r[2] and r[3] <= p1)]
        else:
            lst[:] = [r for r in lst if not (not r[5] and r[4].eng == op.eng and r[4].dma_key is None
                                             and op.dma_key is None
                                             and r[0] == lo and r[1] == hi and r[2] == p0 and r[3] == p1)]
        lst.append((lo, hi, p0, p1, op, is_write))

    def add(self, eng, fn, reads=(), writes=(), dma_key=None):
        op = Op()
        op.eng = eng
        op.fn = fn
        op.sig = False
        op.sigval = 0
        op.dma_key = dma_key
        op.dma_val = 0
        rregs = [g for g in (self.region(a) for a in reads) if g is not None]
        wregs = [g for g in (self.region(a) for a in writes) if g is not None]
        deps = {}
        for g in rregs:
            self._scan(g, False, deps, eng)
        for g in wregs:
            self._scan(g, True, deps, eng)
        waits = []
        for p, kind in deps.items():
            if p.dma_key is not None:
                waits.append(("dma", p.dma_key, 16 * self.dma_cnt[p.dma_key]))
            else:
                if p.eng == eng and eng == "pe":
                    continue
                p.sig = True
                waits.append(("eng", p))
        op.waits = waits
        for g in rregs:
            self._record(g, op, False)
        for g in wregs:
            self._record(g, op, True)
        if dma_key is not None:
            self.dma_cnt[dma_key] = self.dma_cnt.get(dma_key, 0) + 1
            op.dma_val = 16 * self.dma_cnt[dma_key]
        self.streams[eng].append(op)
        self.nops += 1
        return op

    def mm(self, out, lhsT, rhs, start, stop):
        self.add("pe", lambda e: e.matmul(out, lhsT=lhsT, rhs=rhs, start=start, stop=stop),
                 reads=[lhsT, rhs], writes=[out])

    def tr(self, out, in_, ident):
        self.add("pe", lambda e: e.transpose(out=out, in_=in_, identity=ident),
                 reads=[in_, ident], writes=[out])

    def act(self, out, in_, func, scale=1.0, bias=0.0, accum=None):
        reads = [in_]
        if not isinstance(scale, (int, float)):
            reads.append(scale)
        if not isinstance(bias, (int, float)):
            reads.append(bias)
        writes = [out] + ([accum] if accum is not None else [])
        kw = {}
        if accum is not None:
            kw["accum_out"] = accum
        self.add("act", lambda e: e.activation(out=out, in_=in_, func=func, scale=scale, bias=bias, **kw),
                 reads=reads, writes=writes)

    def rsqrt_act(self, out, in_, scale, eps):
        self.act(out, in_, AF.Ln, scale=scale, bias=eps)
        self.act(out, out, AF.Exp, scale=-0.5)

    def ts(self, eng, out, in0, s1, op0, s2=None, op1=None):
        reads = [in0]
        if not isinstance(s1, (int, float)):
            reads.append(s1)
        if s2 is not None and not isinstance(s2, (int, float)):
            reads.append(s2)
        kw = {}
        if op1 is not None:
            kw["op1"] = op1
        self.add(eng, lambda e: e.tensor_scalar(out=out, in0=in0, scalar1=s1, scalar2=s2, op0=op0, **kw),
                 reads=reads, writes=[out])

    def tt(self, eng, out, in0, in1, op):
        self.add(eng, lambda e: e.tensor_tensor(out=out, in0=in0, in1=in1, op=op),
                 reads=[in0, in1], writes=[out])

    def stt(self, out, in0, scalar, in1, op0, op1):
        reads = [in0, in1]
        if not isinstance(scalar, (int, float)):
            reads.append(scalar)
        self.add("dve", lambda e: e.scalar_tensor_tensor(out=out, in0=in0, scalar=scalar, in1=in1, op0=op0, op1=op1),
                 reads=reads, writes=[out])

    def copy(self, eng, out, in_):
        if eng == "act":
            self.add("act", lambda e: e.activation(out=out, in_=in_, func=AF.Copy), reads=[in_], writes=[out])
        else:
            self.add(eng, lambda e: e.tensor_copy(out=out, in_=in_), reads=[in_], writes=[out])

    def recip(self, out, in_, on_dve=False):
        if RECIP_ACT and not on_dve:
            self.act(out, in_, AF.Ln)
            self.act(out, out, AF.Exp, scale=-1.0)
        else:
            self.add("dve", lambda e: e.reciprocal(out=out, in_=in_), reads=[in_], writes=[out])

    def memset(self, eng, out, val):
        self.add(eng, lambda e: e.memset(out, val), writes=[out])

    def dma(self, out, in_, key, reads=(), writes=(), slow=False):
        kw = {"allow_slow_non_contiguous": True} if slow else {}
        self.add("sp", lambda e: e.dma_start(out=out, in_=in_, **kw), reads=reads, writes=writes, dma_key=key)

    def final_wait(self):
        op = Op()
        op.eng = "sp"
        op.fn = None
        op.sig = False
        op.sigval = 0
        op.dma_key = None
        op.dma_val = 0
        op.waits = [("dma", k, 16 * c) for k, c in self.dma_cnt.items()]
        self.streams["sp"].append(op)

    def emit(self):
        nc = self.nc
        es = self.es
        esem = {e: es.enter_context(nc.semaphore(f"{self.tag}_e_{e}")) for e in self.ENGS if e != "sp"}
        dsem = {}
        for i, k in enumerate(self.dma_cnt):
            dsem[k] = es.enter_context(nc.semaphore(f"{self.tag}_d{i}"))
        for e, ops in self.streams.items():
            n = 0
            for op in ops:
                if op.dma_key is None and op.sig:
                    n += 1
                    op.sigval = n

        def run(ename, eng):
            waited = {}
            for op in self.streams[ename]:
                for w in op.waits:
                    if w[0] == "dma":
                        sem, val, sid = dsem[w[1]], w[2], ("d", w[1])
                    else:
                        p = w[1]
                        sem, val, sid = esem[p.eng], p.sigval, ("e", p.eng)
                    if val > waited.get(sid, 0):
                        eng.wait_ge(sem, val)
                        waited[sid] = val
                if op.fn is None:
                    continue
                inst = op.fn(eng)
                if op.dma_key is not None:
                    inst.then_inc(dsem[op.dma_key], 16)
                elif op.sig:
                    inst.then_inc(esem[ename], 1)

        with nc.Block() as block:
            @block.tensor
            def _(e):
                run("pe", e)

            @block.scalar
            def _(e):
                run("act", e)

            @block.vector
            def _(e):
                run("dve", e)

            @block.gpsimd
            def _(e):
                run("pool", e)

            @block.sync
            def _(e):
                run("sp", e)


class BufPool:
    def __init__(self, items):
        self.free = list(items)

    def get(self):
        assert self.free, "buffer pool exhausted (emission-order bug)"
        return self.free.pop(0)

    def put(self, x):
        self.free.append(x)

    def take(self, x):
        for i, f in enumerate(self.free):
            if f is x:
                return self.free.pop(i)
        raise AssertionError("requested buffer not free (emission-order bug)")


class WStream:
    def __init__(self, sch, ring, units):
        self.sch = sch
        self.ring = ring
        self.units = units
        self.R = ring.shape[1]
        self.freeslots = list(range(self.R))
        self.next_load = 0
        self.next_get = 0
        self.loaded = {}

    def _pump(self):
        while self.freeslots and self.next_load < len(self.units):
            s = self.freeslots.pop(0)
            src = self.units[self.next_load]
            n = src.shape[1]
            self.sch.dma(out=self.ring[:, s, 0:n], in_=src, key=("w", s), writes=[self.ring[:, s, :]])
            self.loaded[self.next_load] = s
            self.next_load += 1

    def get(self):
        self._pump()
        i = self.next_get
        self.next_get += 1
        assert i in self.loaded, "weight ring too small for simultaneous units"
        s = self.loaded[i]
        return (i, self.ring[:, s, :])

    def done(self, h):
        s = self.loaded.pop(h[0])
        self.freeslots.append(s)
        self._pump()


def build_program(nseq, S, debug=(), stop=None):
    NT = S // 128
    NB = S // QB
    assert S % QB == 0
    nc = bass.Bass("TRN2", target_bir_lowering=False)

    def din(name, shape):
        return nc.dram_tensor(name, list(shape), F32, kind="ExternalInput").ap()

    x_d = din("x", [nseq, S, D])
    g_mix = din("g_mix", [D])
    w_in = din("w_in", [D, INC])
    g_q_lat = din("g_q_lat", [QL])
    w_uq = din("w_uq", [QL, H * QKH])
    g_kv_lat = din("g_kv_lat", [KVL])
    w_ukv = din("w_ukv", [KVL, H * 256])
    g_qk_q = din("g_qk_q", [QKH])
    g_qk_k = din("g_qk_k", [QKH])
    w_o_attn = din("w_o_attn", [D, D])
    w_dw = din("w_dw", [CW, CD])
    b_dw = din("b_dw", [CD])
    g_conv_ln = din("g_conv_ln", [CD])
    b_conv_ln = din("b_conv_ln", [CD])
    w_o_conv = din("w_o_conv", [CD, D])
    w_out = din("w_out", [D, D])
    g_ffn = din("g_ffn", [D])
    w_ff1 = din("w_ff1", [D, DFF])
    w_ff2 = din("w_ff2", [DFF, D])
    cos_d = din("rope_cos", [S, 64])
    sin_d = din("rope_sin", [S, 64])
    y_d = nc.dram_tensor("y", [nseq, S, D], F32, kind="ExternalOutput").ap()

    def scr(name, shape, dt=BF16):
        return nc.dram_tensor(name, list(shape), dt, kind="Internal").ap()

    sA1 = scr("sA1", [128, 8 * 320])
    sA2 = scr("sA2", [2, 128, 4096])
    sA3 = scr("sA3", [128, 4096])
    sQ1 = scr("sQ1", [128, 2048])
    sQ2 = scr("sQ2", [128, 3072])
    sM = scr("sM", [8, 128, 3584])
    sO = scr("sO", [2, 128, 4096])
    sF1 = scr("sF1", [8, 128, 4096])
    sF2 = scr("sF2", [8, 128, 4096])
    sTK = scr("sTK", [128, NT, 2, 64], F32)
    sTQ = scr("sTQ", [128, NT, 2, 64], F32)

    dbg_out = {}

    with ExitStack() as es:
        E = es.enter_context
        NSTG = 6
        stg = E(nc.sbuf_tensor("stg", [128, NSTG, 4096], F32))
        stb = E(nc.sbuf_tensor("stb", [128, NSTG, 4096], BF16))
        gc = E(nc.sbuf_tensor("gc", [128, 20], F32))
        tcs = E(nc.sbuf_tensor("tcs", [128, 2, NT, 64], F32))
        gro = E(nc.sbuf_tensor("gro", [128, 4, 64], F32))
        tqk = E(nc.sbuf_tensor("tqk", [128, 2, NT, 2, 64], F32))
        sch = Sched(nc, es, "p")

        sch.dma(out=gc[:, 0:8], in_=g_mix.rearrange("(k p) -> p k", p=128), key="gc", writes=[gc[:]], slow=True)
        sch.dma(out=gc[:, 8:16], in_=g_ffn.rearrange("(k p) -> p k", p=128), key="gc", writes=[gc[:]], slow=True)
        sch.dma(out=gc[:, 16:18], in_=g_q_lat.rearrange("(k p) -> p k", p=128), key="gc", writes=[gc[:]], slow=True)
        sch.dma(out=gc[:, 18:20], in_=g_kv_lat.rearrange("(k p) -> p k", p=128), key="gc", writes=[gc[:]], slow=True)
        sch.dma(out=tcs[:, 0, :, :], in_=cos_d.rearrange("(n p) d -> p n d", p=128), key="tc0", writes=[tcs[:, 0, :, :]])
        sch.dma(out=tcs[:, 1, :, :], in_=sin_d.rearrange("(n p) d -> p n d", p=128), key="tc1", writes=[tcs[:, 1, :, :]])
        for qi, gvec in enumerate((g_qk_q, g_qk_k)):
            sch.dma(out=gro[:, 2 * qi, :], in_=gvec[128:192].partition_broadcast(128), key="gro", writes=[gro[:]])
            sch.dma(out=gro[:, 2 * qi + 1, 0:32], in_=gvec[160:192].partition_broadcast(128), key="gro", writes=[gro[:]])
            sch.dma(out=gro[:, 2 * qi + 1, 32:64], in_=gvec[128:160].partition_broadcast(128), key="gro", writes=[gro[:]])
        for qi, dst in enumerate((sTQ, sTK)):
            for cs in range(2):
                sch.tt("dve", tqk[:, qi, :, cs, :], tcs[:, cs, :, :],
                       gro[:, 2 * qi + cs:2 * qi + cs + 1, :].broadcast_to([128, NT, 64]), ALU.mult)
            sch.dma(out=dst, in_=tqk[:, qi, :, :, :], key=("tqk", qi), reads=[tqk[:, qi, :, :, :]])

        cnt = [0]

        preps = []

        def prep(src, ncols, scale, stores, src_view=None):
            i = cnt[0] % NSTG
            cnt[0] += 1
            ceng = ("act", "dve", "act", "pool")[cnt[0] % 4]

            def do_load():
                dst = stg[:, i, 0:ncols] if src_view is None else src_view(stg[:, i, 0:ncols])
                sch.dma(out=dst, in_=src, key=("stg", i), writes=[stg[:, i, :]])

            def do_cast_store():
                if scale is not None:
                    sch.ts("dve", stb[:, i, 0:ncols], stg[:, i, 0:ncols], scale, ALU.mult)
                else:
                    sch.copy(ceng, stb[:, i, 0:ncols], stg[:, i, 0:ncols])
                for sv, dv in stores(stb[:, i, 0:ncols]):
                    sch.dma(out=dv, in_=sv, key=("stb", i), reads=[stb[:, i, :]])
            preps.append((do_load, do_cast_store))

        sA2v = sA2.rearrange("u p (j g k n) -> p u j g k n", j=2, g=2, k=8)
        sMv = sM.rearrange("c p (b n) -> p c b n", n=128)
        sF1v = sF1.rearrange("u p (k n) -> p u k n", k=8)
        for k in range(8):
            def st_win(v, k=k):
                out = [(v[:, 0:256], sQ1[:, k * 256:(k + 1) * 256]),
                       (v[:, 256:576], sA1[:, k * 320:(k + 1) * 320])]
                for g in range(2):
                    for u in range(2):
                        out.append((v[:, 576 + g * 512 + u * 256:576 + g * 512 + (u + 1) * 256].rearrange("p (j n) -> p j n", j=2),
                                    sA2v[:, u, :, g, k, :]))
                for g in range(2):
                    out.append((v[:, 1600 + g * 1024:1600 + (g + 1) * 1024].rearrange("p (c n) -> p c n", n=128),
                                sMv[:, :, 12 + 8 * g + k, :]))
                return out
            prep(w_in[k * 128:(k + 1) * 128, :], INC, gc[:, k:k + 1], st_win)
        for k in range(2):
            prep(w_uq[k * 128:(k + 1) * 128, :], 1536, gc[:, 16 + k:17 + k],
                 lambda v, k=k: [(v, sQ2[:, k * 1536:(k + 1) * 1536])])
        for k in range(2):
            def st_ukv(v, k=k):
                v4 = v.rearrange("p (h g n) -> p h g n", h=8, g=2)
                d4 = sA3[:, k * 2048:(k + 1) * 2048].rearrange("p (g h n) -> p g h n", g=2, h=8)
                return [(v4[:, :, g, :], d4[:, g, :, :]) for g in range(2)]
            prep(w_ukv[k * 128:(k + 1) * 128, :], 2048, gc[:, 18 + k:19 + k], st_ukv)
        for k in range(8):
            prep(w_o_attn[k * 128:(k + 1) * 128, :], 1024, None,
                 lambda v, k=k: [(v.rearrange("p (c n) -> p c n", n=128), sMv[:, :, k, :])])
        for k in range(4):
            prep(w_o_conv[k * 128:(k + 1) * 128, :], 1024, None,
                 lambda v, k=k: [(v.rearrange("p (c n) -> p c n", n=128), sMv[:, :, 8 + k, :])])
        sOv = sO.rearrange("h p (k n) -> p h k n", k=8)
        for k in range(8):
            prep(w_out[k * 128:(k + 1) * 128, :], 1024, None,
                 lambda v, k=k: [(v.rearrange("p (h n) -> p h n", h=2), sOv[:, :, k, :])])
        for k in range(8):
            prep(w_ff1[k * 128:(k + 1) * 128, :], 4096, gc[:, 8 + k:9 + k],
                 lambda v, k=k: [(v.rearrange("p (u n) -> p u n", u=8), sF1v[:, :, k, :])])
        for u in range(8):
            prep(w_ff2[u * 512:(u + 1) * 512, :].rearrange("(kk p) n -> p kk n", p=128), 4096, None,
                 lambda v, u=u: [(v, sF2[u])],
                 src_view=lambda a: a.rearrange("p (kk n) -> p kk n", kk=4))
        ahead = NSTG - 1
        for idx in range(len(preps) + ahead):
            if idx < len(preps):
                preps[idx][0]()
            if idx >= ahead:
                preps[idx - ahead][1]()
        sch.final_wait()
        sch.emit()

    with ExitStack() as es:
        E = es.enter_context

        def sb(name, shape, dt):
            return E(nc.sbuf_tensor(name, list(shape), dt))

        KTn = sb("KTn", [128, 8, S], BF16)
        KTr = sb("KTr", [128, 2, S], BF16)
        Vt = sb("Vt", [128, NT, 1024], BF16)
        uT = sb("uT", [128, 4, S + 32], BF16)
        rk = sb("rk", [128, NT, 8], F32)
        wring = sb("wring", [128, RING, 4096], BF16)
        big = sb("big", [128, 16384], BF16)
        xT = sb("xT", [128, 8, QB], BF16)
        x1buf = sb("x1buf", [128, 4, 1024], F32)
        xn = sb("xn", [128, 2, 1024], BF16)
        latn = sb("latn", [128, 4, 256], BF16)
        latT = sb("latT", [128, 2, QB], BF16)
        tb = sb("tb", [128, 2048], BF16)
        qr = sb("qr", [128, 8, 64], BF16)
        krd = sb("krd", [128, 4, 128], BF16)
        tabs = sb("tabs", [128, 4, 2, 64], F32)
        gm = sb("gm", [128, 4, QB], F32)
        gsb = gm[:, 0:2, :]
        mtmp = gm[:, 2:4, :]
        cacc = gm
        psm = sb("psm", [128, 4, QB], BF16) if QUAD else None
        qlT_all = sb("qlT_all", [128, 2, S], BF16)
        latq = sb("latq", [128, 4, 256], BF16)
        rc = tb[:, 0:1024].bitcast(F32)
        lnT = sb("lnT", [128, 2, 512], BF16)
        junk = lnT[:, :, :].rearrange("p a n -> p (a n)")
        st = sb("st", [128, 256], F32)
        cst = sb("cst", [128, 64], F32)
        wdc = sb("wdc", [128, 4, CW], F32)
        ident = sb("ident", [128, 128], BF16)
        identf = sb("identf", [128, 128], F32)
        ones = sb("ones", [128, 128], BF16)
        psum = E(nc.psum_tensor("psum", [128, 8, 512], F32))

        hT = big[:, :].rearrange("p (k n) -> p k n", n=QB)
        QTn = big[:, 0:4096].rearrange("p (h n) -> p h n", n=QB)
        QTr = big[:, 4096:6144].rearrange("p (h n) -> p h n", n=QB)
        pring_v = big[:, 6144:10240].rearrange("p (h n) -> p h n", n=QB)
        attnT = big[:, 10240:14336].rearrange("p (h n) -> p h n", n=QB)
        uact = big[:, 14336:16384].rearrange("p (h n) -> p h n", n=QB)
        mT = QTn

        ss4 = st[:, 0:4]
        tmp4 = st[:, 4:8]
        rs4 = st[:, 8:12]
        ssQA = st[:, 216:224]
        tmpQA = st[:, 224:232]
        rsQA = st[:, 232:240]
        ssQ = ssQA[:, 0:4]
        ssA = ssQA[:, 4:8]
        rsQ = rsQA[:, 0:4]
        rsA = rsQA[:, 4:8]
        sskpe = st[:, 24:28]
        ssk = st[:, 32:64].rearrange("p (t h) -> p t h", h=8)
        zk = st[:, 64:96].rearrange("p (t h) -> p t h", h=8)
        ssq = st[:, 96:128].rearrange("p (t h) -> p t h", h=8)
        tmpq = st[:, 128:160].rearrange("p (t h) -> p t h", h=8)
        rq = st[:, 160:192].rearrange("p (t h) -> p t h", h=8)
        lsum = st[:, 192:196]
        lssq = st[:, 196:200]
        lmean = st[:, 200:204]
        lmsq = st[:, 204:208]
        lvar = st[:, 208:212]
        lrs = st[:, 212:216]
        mhalf = cst[:, 0:32]
        gqn = cst[:, 32:33]
        gkn = cst[:, 33:34]
        gln = cst[:, 36:40]
        bln = cst[:, 40:44]
        bdw = cst[:, 44:48]

        sch = Sched(nc, es, "m")
        pbank = [psum[:, i, :] for i in range(8)]
        ps = BufPool(pbank)
        pring = BufPool([pring_v[:, i, :] for i in range(8)])

        units = []
        for s in range(nseq):
            for b in range(NB):
                units += [sQ1, sA1, sA2[0], sA2[1], sA3]
            for b in range(NB):
                units += [sQ2] + [sM[c] for c in range(8)] + [sO[0], sO[1]]
                units += [sF1[u] for u in range(8)] + [sF2[u] for u in range(8)]
        W = WStream(sch, wring, units)

        def dbg(name, ap, dt=None):
            if name not in debug or name in dbg_out:
                return
            dt = dt or ap.dtype
            t = nc.dram_tensor("dbg_" + name, list(ap.shape), dt, kind="ExternalOutput").ap()
            dbg_out[name] = t
            sch.dma(out=t, in_=ap, key="dbg", reads=[ap])

        sch.memset("pool", ident[:], 0.0)
        sch.add("pool", lambda e: e.affine_select(out=ident[:], in_=ident[:], compare_op=ALU.not_equal, fill=1.0,
                                                  base=0, pattern=[[-1, 128]], channel_multiplier=1),
                reads=[ident[:]], writes=[ident[:]])
        sch.memset("pool", identf[:], 0.0)
        sch.add("pool", lambda e: e.affine_select(out=identf[:], in_=identf[:], compare_op=ALU.not_equal, fill=1.0,
                                                  base=0, pattern=[[-1, 128]], channel_multiplier=1),
                reads=[identf[:]], writes=[identf[:]])
        sch.memset("pool", ones[:], 1.0)
        sch.memset("pool", cst[:, 0:32], -0.5)
        sch.memset("pool", uT[:], 0.0)
        sch.memset("pool", KTr[:], 0.0)
        sch.dma(out=cst[:, 32:33], in_=g_qk_q[0:128].rearrange("(p o) -> p o", o=1), key="cst", writes=[cst[:, 32:64]], slow=True)
        sch.dma(out=cst[:, 33:34], in_=g_qk_k[0:128].rearrange("(p o) -> p o", o=1), key="cst", writes=[cst[:, 32:64]], slow=True)
        sch.dma(out=cst[:, 36:40], in_=g_conv_ln.rearrange("(i p) -> p i", p=128), key="cst", writes=[cst[:, 32:64]], slow=True)
        sch.dma(out=cst[:, 40:44], in_=b_conv_ln.rearrange("(i p) -> p i", p=128), key="cst", writes=[cst[:, 32:64]], slow=True)
        sch.dma(out=cst[:, 44:48], in_=b_dw.rearrange("(i p) -> p i", p=128), key="cst", writes=[cst[:, 32:64]], slow=True)
        wdr = mtmp[0:CW, 0, :]
        sch.dma(out=wdr, in_=w_dw, key="wdr", writes=[wdr])
        bk = ps.get()
        for i in range(4):
            sch.tr(bk[:, i * CW:(i + 1) * CW], mtmp[0:CW, 0, i * 128:(i + 1) * 128], identf[0:CW, 0:CW])
        sch.copy("dve", wdc[:], bk[:, 0:4 * CW].rearrange("p (i j) -> p i j", i=4))
        ps.put(bk)

        def load_x(s, b):
            for t in range(4):
                r0 = b * QB + t * 128
                sch.dma(out=x1buf[:, t, :], in_=x_d[s, r0:r0 + 128, :], key=("x", t), writes=[x1buf[:, t, :]])

        def norm_stats():
            for t in range(4):
                sch.act(junk[:, :], x1buf[:, t, :], AF.Square, accum=ss4[:, t:t + 1])
            sch.rsqrt_act(rs4, ss4, 1.0 / D, EPS)

        def norm_stats_tile(t):
            sch.act(junk[:, :], x1buf[:, t, :], AF.Square, accum=ss4[:, t:t + 1])
            sch.rsqrt_act(rs4[:, t:t + 1], ss4[:, t:t + 1], 1.0 / D, EPS)

        def norm_tile(t):
            i = t % 2
            sch.ts("dve", xn[:, i, :], x1buf[:, t, :], rs4[:, t:t + 1], ALU.mult)
            bT = ps.get()
            bTb = bT.bitcast(BF16)
            for k in range(8):
                sch.tr(bTb[:, k * 128:(k + 1) * 128], xn[:, i, k * 128:(k + 1) * 128], ident[:])
            sch.copy("act" if t % 2 else "dve", xT[:, :, t * 128:(t + 1) * 128],
                     bTb.rearrange("p (k n) -> p k n", n=128))
            ps.put(bT)

        def norm_T():
            norm_stats()
            for t in range(4):
                norm_tile(t)

        def phaseA(s, b):
            tok0 = b * QB
            sch.dma(out=tabs[:], in_=sTK[:, b * 4:(b + 1) * 4, :, :], key="tab", writes=[tabs[:]])
            has_next = b + 1 < NB
            if b == 0:
                load_x(s, 0)
                norm_T()
                if has_next:
                    load_x(s, 1)
            wQ1 = W.get()
            wA1 = W.get()
            banks = []
            bky = ps.get()
            for t in range(4):
                bk = ps.get()
                banks.append(bk)
                xs = xT[:, :, t * 128:(t + 1) * 128]
                for k in range(8):
                    sch.mm(bk[:, 0:256], xs[:, k, :], wQ1[1][:, k * 256:(k + 1) * 256], k == 0, k == 7)
                for k in range(8):
                    sch.mm(bk[:, 256:512], xs[:, k, :], wA1[1][:, k * 320:k * 320 + 256], k == 0, k == 7)
                for k in range(8):
                    sch.mm(bky[:, t * 64:(t + 1) * 64], xs[:, k, :], wA1[1][:, k * 320 + 256:(k + 1) * 320], k == 0, k == 7)
                sch.act(junk[:, 0:256], bk[:, 0:256], AF.Square, accum=ssQ[:, t:t + 1])
                sch.act(junk[:, 256:512], bk[:, 256:512], AF.Square, accum=ssA[:, t:t + 1])
                sch.act(junk[:, 512:576], bky[:, t * 64:(t + 1) * 64], AF.Square, accum=sskpe[:, t:t + 1])
            W.done(wQ1)
            W.done(wA1)
            sch.rsqrt_act(rsQA, ssQA, 1.0 / KVL, EPS)
            t1 = mtmp[:, 0, 0:256].rearrange("p (t d) -> p t d", d=64)
            t2 = mtmp[:, 0, 256:512].rearrange("p (t d) -> p t d", d=64)
            ky4 = bky[:, 0:256].rearrange("p (t d) -> p t d", d=64)
            sch.tt("dve", t1, ky4, tabs[:, :, 0, :], ALU.mult)
            sch.tt("dve", t2[:, :, 0:32], ky4[:, :, 32:64], tabs[:, :, 1, 0:32], ALU.mult)
            sch.tt("dve", t2[:, :, 32:64], ky4[:, :, 0:32], tabs[:, :, 1, 32:64], ALU.mult)
            ps.put(bky)
            krd4 = krd[:, :, :].rearrange("p t (r d) -> p t r d", r=2)
            sch.tt("dve", krd4[:, :, 0, :], t1, t2, ALU.add)
            sch.tt("dve", krd4[:, :, 1, :], t1, t2, ALU.add)
            for t in range(4):
                bk = banks[t]
                sch.ts("dve", latn[:, t, :], bk[:, 256:512], rsA[:, t:t + 1], ALU.mult)
                sch.ts("dve", latq[:, t, :], bk[:, 0:256], rsQ[:, t:t + 1], ALU.mult)
                ps.put(bk)

            def conv_pair(wA2, j, i):
                ba = ps.get()
                bg = ps.get()
                for k in range(8):
                    o = ((j * 2 + 1) * 8 + k) * 128
                    sch.mm(bg[:, :], wA2[1][:, o:o + 128], xT[:, k, :], k == 0, k == 7)
                for k in range(8):
                    o = ((j * 2 + 0) * 8 + k) * 128
                    sch.mm(ba[:, :], wA2[1][:, o:o + 128], xT[:, k, :], k == 0, k == 7)
                sch.act(gsb[:, j, :], bg[:, :], AF.Sigmoid)
                sch.tt("dve", uT[:, i, 16 + tok0:16 + tok0 + QB], ba[:, :], gsb[:, j, :], ALU.mult)
                ps.put(ba)
                ps.put(bg)

            def lat_T(t):
                bT = ps.get()
                bTb = bT.bitcast(BF16)
                sch.tr(bTb[:, 0:128], latn[:, t, 0:128], ident[:])
                sch.tr(bTb[:, 128:256], latn[:, t, 128:256], ident[:])
                sch.tr(bTb[:, 256:384], krd[:, t, :], ident[:])
                sch.tr(bTb[:, 384:512], latq[:, t, 0:128], ident[:])
                sch.tr(bTb[:, 512:640], latq[:, t, 128:256], ident[:])
                sch.copy("act", latT[:, :, t * 128:(t + 1) * 128], bTb[:, 0:256].rearrange("p (c n) -> p c n", n=128))
                sch.copy("act", KTr[0:64, 0, tok0 + t * 128:tok0 + (t + 1) * 128], bTb[0:64, 256:384])
                sch.copy("act", KTr[64:128, 1, tok0 + t * 128:tok0 + (t + 1) * 128], bTb[64:128, 256:384])
                sch.copy("dve", qlT_all[:, :, tok0 + t * 128:tok0 + (t + 1) * 128],
                         bTb[:, 384:640].rearrange("p (c n) -> p c n", n=128))
                ps.put(bT)

            wA2a = W.get()
            conv_pair(wA2a, 0, 0)
            lat_T(0)
            lat_T(1)
            conv_pair(wA2a, 1, 1)
            W.done(wA2a)
            lat_T(2)
            lat_T(3)
            wA2b = W.get()
            wA3 = W.get()

            def kv_front(t):
                tile = b * 4 + t
                bks = [ps.get(), ps.get()]
                bvs = [ps.get(), ps.get()]
                for hf in range(2):
                    for c in range(2):
                        sch.mm(bks[hf][:, :], latT[:, c, t * 128:(t + 1) * 128],
                               wA3[1][:, c * 2048 + hf * 512:c * 2048 + (hf + 1) * 512], c == 0, c == 1)
                for hf in range(2):
                    for c in range(2):
                        sch.mm(bvs[hf][:, :], latT[:, c, t * 128:(t + 1) * 128],
                               wA3[1][:, c * 2048 + 1024 + hf * 512:c * 2048 + 1024 + (hf + 1) * 512], c == 0, c == 1)
                for h in range(8):
                    sch.act(junk[:, h * 128:(h + 1) * 128], bks[h // 4][:, (h % 4) * 128:(h % 4 + 1) * 128], AF.Square,
                            accum=ssk[:, t, h:h + 1])
                o = (t % 2) * 1024
                for hf in range(2):
                    sch.copy("dve", tb[:, o + hf * 512:o + (hf + 1) * 512], bks[hf][:, :])
                    sch.copy("act", Vt[:, tile, hf * 512:(hf + 1) * 512], bvs[hf][:, :])
                for x in bks + bvs:
                    ps.put(x)

            def kv_back(t):
                o = (t % 2) * 1024
                bT = ps.get()
                bTb = bT.bitcast(BF16)
                for h in range(8):
                    sch.tr(bTb[:, h * 128:(h + 1) * 128], tb[:, o + h * 128:o + (h + 1) * 128], ident[:])
                sch.ts("dve", KTn[:, :, tok0 + t * 128:tok0 + (t + 1) * 128],
                       bTb.rearrange("p (h n) -> p h n", n=128), gkn, ALU.mult)
                ps.put(bT)

            conv_pair(wA2b, 0, 2)
            kv_front(0)
            conv_pair(wA2b, 1, 3)
            W.done(wA2b)
            if has_next:
                norm_stats()
            kv_front(1)
            kv_back(0)
            if has_next:
                norm_tile(0)
                norm_tile(1)
            kv_front(2)
            kv_back(1)
            if has_next:
                norm_tile(2)
                norm_tile(3)
                if b + 2 < NB:
                    load_x(s, b + 2)
            kv_front(3)
            kv_back(2)
            kv_back(3)
            W.done(wA3)
            for t in range(4):
                sch.ts("dve", zk[:, t, :], ssk[:, t, :], sskpe[:, t:t + 1], ALU.add, QKH * EPS, ALU.add)
            sch.rsqrt_act(rk[:, b * 4:(b + 1) * 4, :], zk, 1.0, 0.0)

        def phaseBC(s, b):
            tok0 = b * QB
            sch.dma(out=tabs[:], in_=sTQ[:, b * 4:(b + 1) * 4, :, :], key="tab", writes=[tabs[:]])
            wQ2 = W.get()
            tb3 = tb[:, 0:1536].rearrange("p (h d) -> p h d", d=QKH)
            xnf = xn[:, :, :].rearrange("p a n -> p (a n)").bitcast(F32)
            t1 = xnf[:, 0:512].rearrange("p (h d) -> p h d", d=64)
            t2 = xnf[:, 512:1024].rearrange("p (h d) -> p h d", d=64)
            qb = {}
            qsets = [(0, 1, 2), (3, 4, 5)]

            conv_ops = [(j, i) for j in range(CW) for i in range(4)]
            conv_pos = [0]

            def conv_emit(nmax):
                while nmax > 0 and conv_pos[0] < len(conv_ops):
                    j, i = conv_ops[conv_pos[0]]
                    conv_pos[0] += 1
                    nmax -= 1
                    src = uT[:, i, tok0 + j + 1:tok0 + j + 1 + QB]
                    if j == 0:
                        sch.ts("dve", cacc[:, i, :], src, wdc[:, i, 0:1], ALU.mult, bdw[:, i:i + 1], ALU.add)
                    else:
                        sch.stt(cacc[:, i, :], src, wdc[:, i, j:j + 1], cacc[:, i, :], ALU.mult, ALU.add)

            def q_stage1(t):
                idx = qsets[t % 2]
                bq = [ps.take(pbank[i]) for i in idx]
                qv = psum[:, idx[0]:idx[0] + 3, :].rearrange("p b n -> p (b n)")
                qb[t] = (bq, qv)
                for j in range(3):
                    for c in range(2):
                        sch.mm(bq[j][:, :], qlT_all[:, c, tok0 + t * 128:tok0 + (t + 1) * 128],
                               wQ2[1][:, c * 1536 + j * 512:c * 1536 + (j + 1) * 512], c == 0, c == 1)
                for h in range(8):
                    jo = (h % 5) * QKH
                    sch.act(junk[:, jo:jo + QKH], qv[:, h * QKH:(h + 1) * QKH], AF.Square, accum=ssq[:, t, h:h + 1])
                sch.rsqrt_act(rq[:, t, :], ssq[:, t, :], 1.0 / QKH, EPS)

            def q_stage2(t):
                bq, qv = qb.pop(t)
                for h in range(8):
                    if h % 2 == 0:
                        sch.ts("dve", tb[:, h * QKH:(h + 1) * QKH], qv[:, h * QKH:(h + 1) * QKH], rq[:, t, h:h + 1], ALU.mult)
                    else:
                        sch.act(tb[:, h * QKH:(h + 1) * QKH], qv[:, h * QKH:(h + 1) * QKH], AF.Copy, scale=rq[:, t, h:h + 1])
                for x in bq:
                    ps.put(x)
                cosb = tabs[:, t, 0:1, :].broadcast_to([128, 8, 64])
                sch.tt("dve", t1, tb3[:, :, 128:192], cosb, ALU.mult)
                sch.tt("dve", t2[:, :, 0:32], tb3[:, :, 160:192], tabs[:, t, 1:2, 0:32].broadcast_to([128, 8, 32]), ALU.mult)
                sch.tt("dve", t2[:, :, 32:64], tb3[:, :, 128:160], tabs[:, t, 1:2, 32:64].broadcast_to([128, 8, 32]), ALU.mult)
                sch.tt("dve", qr[:, :, :], t1, t2, ALU.add)
                bT = ps.take(pbank[6])
                bTb = bT.bitcast(BF16)
                for h in range(8):
                    sch.tr(bTb[:, h * 128:(h + 1) * 128], tb3[:, h, 0:128], ident[:])
                bT2 = ps.take(pbank[7])
                bT2b = bT2.bitcast(BF16)
                qr2 = qr[:, :, :].rearrange("p h d -> p (h d)")
                for p in range(4):
                    sch.tr(bT2b[:, p * 128:(p + 1) * 128], qr2[:, p * 128:(p + 1) * 128], ident[:])
                conv_emit(4)
                sch.ts("dve", QTn[:, :, t * 128:(t + 1) * 128], bTb.rearrange("p (h n) -> p h n", n=128), gqn, ALU.mult)
                ps.put(bT)
                sch.copy("act", QTr[:, :, t * 128:(t + 1) * 128], bT2b[:, 0:512].rearrange("p (h n) -> p h n", n=128))
                ps.put(bT2)

            q_stage1(0)
            for t in range(4):
                if t + 1 < 4:
                    q_stage1(t + 1)
                q_stage2(t)
            W.done(wQ2)
            dbg("QTn", QTn)
            dbg("QTr", QTr)

            steps = [(h, kt) for h in range(H) for kt in range(NT)]
            LA = 4
            acc = {}
            pslots = {}

            def qk(i):
                h, kt = steps[i]
                bs = ps.get()
                sch.mm(bs[:, :], KTn[:, h, kt * 128:(kt + 1) * 128], QTn[:, h, :], True, False)
                sch.mm(bs[:, :], KTr[:, h % 2, kt * 128:(kt + 1) * 128], QTr[:, h // 2, :], False, True)
                p = pring.get()
                pslots[i] = p
                sch.act(p, bs[:, :], AF.Exp, scale=rk[:, kt, h:h + 1])
                ps.put(bs)

            quad = {}
            qcnt = [0]

            def pv(i):
                h, kt = steps[i]
                if kt == 0:
                    acc[h] = (ps.get(), ps.get())
                ao, asum = acc[h]
                p = pslots.pop(i)
                sch.mm(ao[:, :], Vt[:, kt, h * 128:(h + 1) * 128], p, kt == 0, kt == NT - 1)
                if QUAD:
                    quad.setdefault(h, []).append(p)
                    if kt % 4 == 3:
                        p0, p1, p2, p3 = quad.pop(h)
                        sa = psm[:, (qcnt[0] % 2) * 2, :]
                        sb_ = psm[:, (qcnt[0] % 2) * 2 + 1, :]
                        qcnt[0] += 1
                        sch.tt(QUAD, sa, p0, p1, ALU.add)
                        sch.tt(QUAD, sb_, p2, p3, ALU.add)
                        sch.tt(QUAD, sa, sa, sb_, ALU.add)
                        sch.mm(asum[:, :], ones[:], sa, kt == 3, kt == NT - 1)
                        for x in (p0, p1, p2, p3):
                            pring.put(x)
                else:
                    sch.mm(asum[:, :], ones[:], p, kt == 0, kt == NT - 1)
                    pring.put(p)
                if kt % 4 == 3:
                    conv_emit(5)
                if kt == NT - 1:
                    sch.recip(rc, asum[:, :], on_dve=(h >= H - 2))
                    sch.tt("dve", attnT[:, h, :], ao[:, :], rc, ALU.mult)
                    ps.put(ao)
                    ps.put(asum)
                    del acc[h]

            def ln_part1(t):
                i2 = t % 2
                bk = ps.get()
                for i in range(4):
                    sch.tr(bk[:, i * 128:(i + 1) * 128], cacc[:, i, t * 128:(t + 1) * 128], identf[:])
                sch.act(lnT[:, i2, :], bk[:, :], AF.Identity, accum=lsum[:, t:t + 1])
                sch.act(lnT[:, i2, :], bk[:, :], AF.Square, accum=lssq[:, t:t + 1])
                c = slice(t, t + 1)
                sch.ts("dve", lmean[:, c], lsum[:, c], 1.0 / CD, ALU.mult)
                sch.tt("dve", lmsq[:, c], lmean[:, c], lmean[:, c], ALU.mult)
                sch.ts("dve", lvar[:, c], lssq[:, c], 1.0 / CD, ALU.mult, EPS, ALU.add)
                sch.tt("dve", lvar[:, c], lvar[:, c], lmsq[:, c], ALU.subtract)
                sch.rsqrt_act(lrs[:, c], lvar[:, c], 1.0, 0.0)
                sch.ts("dve", lnT[:, i2, :], bk[:, :], lmean[:, c], ALU.subtract, lrs[:, c], ALU.mult)
                ps.put(bk)

            def ln_part2(t):
                i2 = t % 2
                bT = ps.get()
                bTb = bT.bitcast(BF16)
                for i in range(4):
                    sch.tr(bTb[:, i * 128:(i + 1) * 128], lnT[:, i2, i * 128:(i + 1) * 128], ident[:])
                for i in range(4):
                    sch.act(uact[:, i, t * 128:(t + 1) * 128], bTb[:, i * 128:(i + 1) * 128], AF.Silu,
                            scale=gln[:, i:i + 1], bias=bln[:, i:i + 1])
                ps.put(bT)

            ln_at = {}
            for t in range(4):
                p1 = (H - 2) * NT + LA + 1 + t * (NT // 2)
                ln_at.setdefault(p1, []).append((1, t))
                ln_at.setdefault(p1 + 7, []).append((2, t))
            ln_done = set()

            def ln_run(part, t):
                if (part, t) in ln_done:
                    return
                ln_done.add((part, t))
                if part == 1:
                    if t == 0:
                        conv_emit(10 ** 6)
                    ln_part1(t)
                else:
                    ln_part2(t)
            n = len(steps)
            for i in range(n + LA):
                if i < n:
                    qk(i)
                if i - LA >= 0:
                    pv(i - LA)
                if i == min(LA + 2, n + LA - 1):
                    load_x(s, b)
                if i == min(NT // 2 + LA, n + LA - 1):
                    norm_stats()
                for t in range(4):
                    if i == min(NT + LA + 2 * t, n + LA - 1):
                        norm_tile(t)
                for part, t in ln_at.get(i, ()):
                    ln_run(part, t)
            for t in range(4):
                ln_run(1, t)
                ln_run(2, t)
            dbg("attnT", attnT)
            dbg("cacc", cacc[:])
            dbg("uact", uact)
            for c in range(8):
                wM = W.get()
                bga, bgc, bao, bco = ps.get(), ps.get(), ps.get(), ps.get()
                for k in range(8):
                    sch.mm(bga[:, :], wM[1][:, (12 + k) * 128:(13 + k) * 128], xT[:, k, :], k == 0, k == 7)
                for k in range(8):
                    sch.mm(bgc[:, :], wM[1][:, (20 + k) * 128:(21 + k) * 128], xT[:, k, :], k == 0, k == 7)
                for k in range(8):
                    sch.mm(bao[:, :], wM[1][:, k * 128:(k + 1) * 128], attnT[:, k, :], k == 0, k == 7)
                for k in range(4):
                    sch.mm(bco[:, :], wM[1][:, (8 + k) * 128:(9 + k) * 128], uact[:, k, :], k == 0, k == 3)
                W.done(wM)
                sch.act(gsb[:, 0, :], bga[:, :], AF.Sigmoid)
                sch.act(gsb[:, 1, :], bgc[:, :], AF.Sigmoid)
                sch.tt("dve", mtmp[:, 0, :], bao[:, :], gsb[:, 0, :], ALU.mult)
                sch.tt("dve", mtmp[:, 1, :], bco[:, :], gsb[:, 1, :], ALU.mult)
                sch.tt("pool", mT[:, c, :], mtmp[:, 0, :], mtmp[:, 1, :], ALU.add)
                for x in (bga, bgc, bao, bco):
                    ps.put(x)
            dbg("mT", mT)
            wO = [W.get(), W.get()]
            for t in range(4):
                for hf in range(2):
                    bk = ps.get()
                    for k in range(8):
                        sch.mm(bk[:, :], mT[:, k, t * 128:(t + 1) * 128], wO[hf][1][:, k * 512:(k + 1) * 512], k == 0, k == 7)
                    sch.tt("dve", x1buf[:, t, hf * 512:(hf + 1) * 512], bk[:, :], x1buf[:, t, hf * 512:(hf + 1) * 512], ALU.add)
                    ps.put(bk)
                norm_stats_tile(t)
                if t >= 2:
                    norm_tile(t - 2)
            norm_tile(2)
            norm_tile(3)
            W.done(wO[0])
            W.done(wO[1])
            dbg("x1", x1buf[:])
            for u in range(8):
                wF = W.get()
                for m in range(4):
                    bk = ps.get()
                    for k in range(8):
                        sch.mm(bk[:, :], wF[1][:, k * 512 + m * 128:k * 512 + (m + 1) * 128], xT[:, k, :], k == 0, k == 7)
                    j = m % 2
                    sch.act(mtmp[:, j, :], bk[:, :], AF.Relu)
                    ps.put(bk)
                    sch.tt("pool", hT[:, u * 4 + m, :], mtmp[:, j, :], mtmp[:, j, :], ALU.mult)
                W.done(wF)
            yb = [ps.get() for _ in range(8)]
            for u in range(8):
                wF = W.get()
                for kk in range(4):
                    k = u * 4 + kk
                    for t in range(4):
                        for hf in range(2):
                            sch.mm(yb[t * 2 + hf][:, :], hT[:, k, t * 128:(t + 1) * 128],
                                   wF[1][:, kk * 1024 + hf * 512:kk * 1024 + (hf + 1) * 512], k == 0, k == 31)
                W.done(wF)
            for t in range(4):
                for hf in range(2):
                    sch.tt("dve", x1buf[:, t, hf * 512:(hf + 1) * 512], yb[t * 2 + hf][:, :],
                           x1buf[:, t, hf * 512:(hf + 1) * 512], ALU.add)
                    ps.put(yb[t * 2 + hf])
                r0 = tok0 + t * 128
                sch.dma(out=y_d[s, r0:r0 + 128, :], in_=x1buf[:, t, :], key=("y", t), reads=[x1buf[:, t, :]])

        if stop in ("prepass", "setup"):
            load_x(0, 0)
            for t in range(4):
                sch.dma(out=y_d[0, t * 128:(t + 1) * 128, :], in_=x1buf[:, t, :], key=("y", t), reads=[x1buf[:, t, :]])
        if stop == "normT":
            load_x(0, 0)
            norm_T()
            dbg("xT", xT[:])
        for s in range(nseq if stop is None or stop[0] in "AB" else 0):
            for b in range(NB):
                phaseA(s, b)
                if s == 0 and b == NB - 1:
                    dbg("KTn", KTn[:])
                    dbg("KTr", KTr[:])
                    dbg("Vt", Vt[:])
                    dbg("uT", uT[:])
                    dbg("rk", rk[:])
            for b in range(NB if (stop is None or stop[0] != "A") else 0):
                phaseBC(s, b)
        sch.final_wait()
        sch.emit()
        nops = sch.nops
    return nc, dbg_out, nops


def rope_tables(S):
    inv_freq = (1.0 / (np.float32(10000.0) ** (np.arange(0, 64, 2, dtype=np.float32) / np.float32(64)))).astype(np.float32)
    pos = np.arange(S, dtype=np.float32)
    ang = pos[:, None] * inv_freq[None, :]
    ang = np.concatenate([ang, ang], axis=-1).astype(np.float32)
    cos = np.cos(ang).astype(np.float32)
    sin = np.sin(ang).astype(np.float32)
    sin_s = sin.copy()
    sin_s[:, 0:32] = -sin_s[:, 0:32]
    return cos, sin_s


WNAMES = ["g_mix", "w_in", "g_q_lat", "w_uq", "g_kv_lat", "w_ukv", "g_qk_q", "g_qk_k", "w_o_attn", "w_dw", "b_dw",
          "g_conv_ln", "b_conv_ln", "w_o_conv", "w_out", "g_ffn", "w_ff1", "w_ff2"]


def weight_map(inputs):
    m = {}
    for n in WNAMES:
        a = np.asarray(inputs[n], dtype=np.float32)
        a = a[0]
        if n == "w_dw":
            a = a.reshape(CW, CD)
        m[n] = np.ascontiguousarray(a)
    return m


_PROG = {}


def kernel(**inputs):
    xp = np.asarray(inputs["x_prompt"], dtype=np.float32)
    xs = np.asarray(inputs["x_sample"], dtype=np.float32)
    S = xp.shape[1]
    nb_p, nb_s = xp.shape[0], xs.shape[0]
    xall = np.concatenate([xp, xs], axis=0)
    ntot = xall.shape[0]
    nseq = ntot // NCORES
    key = (nseq, S)
    if key not in _PROG:
        _PROG[key] = build_program(nseq, S)[0]
    nc = _PROG[key]
    wm = weight_map(inputs)
    cos, sin_s = rope_tables(S)
    in_maps = []
    for c in range(NCORES):
        m = dict(wm)
        m["x"] = np.ascontiguousarray(xall[c * nseq:(c + 1) * nseq])
        m["rope_cos"] = cos
        m["rope_sin"] = sin_s
        in_maps.append(m)
    res = run_bass_kernel_spmd(nc, in_maps, core_ids=list(range(NCORES)))
    yall = np.concatenate([r["y"] for r in res.results], axis=0)
    return (np.ascontiguousarray(yall[:nb_p]), np.ascontiguousarray(yall[nb_p:]))
```
